# Optimizing a Trainium2 kernel written in Bass

```python
import math
import jax, jax.numpy as jnp
from jax import lax
import numpy as np

D_MODEL = 1024
BATCH = 8
SEQ = 2048
DEPTH = 4
DEC_BATCH = 128
DEC_SEQ = 1
PAST_LEN = 2048
PAGE_SIZE = 128

N_MIXERS = 3
N_LAYERS_A = (DEPTH + 2) // 3
N_LAYERS_B = (DEPTH + 1) // 3
N_LAYERS_C = DEPTH // 3

SSM_GROUP_WIDTH = 16
SSM_GROUPS = D_MODEL // SSM_GROUP_WIDTH
SSM_STATE = 64
SSM_DT_MIN = 1e-3
SSM_DT_MAX = 1e-1

SB_HEAD_DIM = 64
SB_HEADS = D_MODEL // SB_HEAD_DIM
Q_BLOCK = 128
SB_BIAS_MIN = -8.0
SB_BIAS_MAX = -5.0

CM_CHUNK = 128
CM_WIDTH = D_MODEL
CM_GROUPS = 8
CM_GROUP_WIDTH = CM_WIDTH // CM_GROUPS

D_FF = (-((-8 * D_MODEL) // (3 * 256))) * 256
EPS = 1e-6

kernel_name = 'hybrid_s5_stickbreak_chunkgmlp_step'


def rms_norm(x, g):
    xf = x.astype(jnp.float32)
    y = xf * lax.rsqrt(jnp.mean(xf * xf, axis=-1, keepdims=True) + EPS)
    return (y * g.astype(jnp.float32)).astype(x.dtype)


def layer_norm(x, g):
    xf = x.astype(jnp.float32)
    mu = jnp.mean(xf, axis=-1, keepdims=True)
    var = jnp.mean(jnp.square(xf - mu), axis=-1, keepdims=True)
    return ((xf - mu) * lax.rsqrt(var + EPS) * g.astype(jnp.float32)).astype(x.dtype)


def swiglu_ffn(x, w_in, w_out):
    gate, up = jnp.split(x @ w_in, 2, axis=-1)
    return (jax.nn.silu(gate) * up) @ w_out


def _complex_affine_combine(earlier, later):
    a1r, a1i, b1r, b1i = earlier
    a2r, a2i, b2r, b2i = later
    return (a2r * a1r - a2i * a1i,
            a2r * a1i + a2i * a1r,
            a2r * b1r - a2i * b1i + b2r,
            a2r * b1i + a2i * b1r + b2i)


def s5_mixer(xn, w_in, lam_re, lam_im, log_dt, b_re, b_im, c_re, c_im, d_skip, w_glu, h0_re=None, h0_im=None):
    f32 = jnp.float32
    bn, length, _ = xn.shape
    u = (xn @ w_in).astype(f32)
    ug = u.reshape(bn, length, SSM_GROUPS, SSM_GROUP_WIDTH)
    lr = lam_re.astype(f32)
    li = lam_im.astype(f32)
    dt = jnp.exp(log_dt.astype(f32))[:, None]
    mag = jnp.exp(lr * dt)
    ar = mag * jnp.cos(li * dt)
    ai = mag * jnp.sin(li * dt)
    den = lr * lr + li * li
    qr = ((ar - 1.0) * lr + ai * li) / den
    qi = (ai * lr - (ar - 1.0) * li) / den
    bur = jnp.einsum('blgp,gnp->lbgn', ug, b_re.astype(f32))
    bui = jnp.einsum('blgp,gnp->lbgn', ug, b_im.astype(f32))
    br = qr * bur - qi * bui
    bi = qr * bui + qi * bur
    shape_a = (length, 1, SSM_GROUPS, SSM_STATE)
    acr, aci, hr, hi = lax.associative_scan(
        _complex_affine_combine,
        (jnp.broadcast_to(ar, shape_a), jnp.broadcast_to(ai, shape_a), br, bi), axis=0)
    if h0_re is not None:
        h0r = h0_re.astype(f32)[None]
        h0i = h0_im.astype(f32)[None]
        hr, hi = hr + acr * h0r - aci * h0i, hi + acr * h0i + aci * h0r
    y = (jnp.einsum('lbgn,gpn->blgp', hr, c_re.astype(f32))
         - jnp.einsum('lbgn,gpn->blgp', hi, c_im.astype(f32)))
    y = y.reshape(bn, length, D_MODEL) + d_skip.astype(f32) * u
    z = jax.nn.gelu(y).astype(xn.dtype)
    val, gate = jnp.split(z @ w_glu, 2, axis=-1)
    return val * jax.nn.sigmoid(gate), hr[-1].astype(xn.dtype), hi[-1].astype(xn.dtype)


def sb_attend(q, k, v, q_pos, k_pos, bias):
    z = (jnp.einsum('bqhd,bkhd->bhqk', q, k).astype(jnp.float32) * (SB_HEAD_DIM ** -0.5)
         + bias.astype(jnp.float32)[None, :, None, None])
    mask = k_pos[None, :] < q_pos[:, None]
    log_not = jnp.where(mask, jax.nn.log_sigmoid(-z), 0.0)
    suffix = lax.cumsum(log_not, axis=3, reverse=True)
    log_w = jax.nn.log_sigmoid(z) + suffix - log_not
    w = jnp.where(mask, jnp.exp(log_w), 0.0).astype(v.dtype)
    return jnp.einsum('bhqk,bkhd->bqhd', w, v)


def sb_project(xn, w_qkv):
    bn, length, _ = xn.shape
    q, k, v = jnp.split(xn @ w_qkv, 3, axis=-1)
    shp = (bn, length, SB_HEADS, SB_HEAD_DIM)
    return q.reshape(shp), k.reshape(shp), v.reshape(shp)


def sb_mixer_prompt(xn, w_qkv, w_out, bias):
    bn, length, _ = xn.shape
    q, k, v = sb_project(xn, w_qkv)
    nb = length // Q_BLOCK
    pos = jnp.arange(length, dtype=jnp.int32)
    qb = q.reshape(bn, nb, Q_BLOCK, SB_HEADS, SB_HEAD_DIM).transpose(1, 0, 2, 3, 4)
    ob = lax.map(lambda blk: sb_attend(blk[0], k, v, blk[1], pos, bias), (qb, pos.reshape(nb, Q_BLOCK)))
    o = ob.transpose(1, 0, 2, 3, 4).reshape(bn, length, D_MODEL)
    return o @ w_out, k, v


def sb_mixer_sample(xn, w_qkv, w_out, bias, pool_k, pool_v, page_table):
    bn, t, _ = xn.shape
    q, k, v = sb_project(xn, w_qkv)
    k_past = pool_k[page_table].reshape(bn, -1, SB_HEADS, SB_HEAD_DIM)
    v_past = pool_v[page_table].reshape(bn, -1, SB_HEADS, SB_HEAD_DIM)
    past = k_past.shape[1]
    kk = jnp.concatenate([k_past, k.astype(k_past.dtype)], axis=1)
    vv = jnp.concatenate([v_past, v.astype(v_past.dtype)], axis=1)
    q_pos = past + jnp.arange(t, dtype=jnp.int32)
    k_pos = jnp.arange(past + t, dtype=jnp.int32)
    o = sb_attend(q, kk, vv, q_pos, k_pos, bias).reshape(bn, t, D_MODEL)
    return o @ w_out, k, v


def chunk_gmlp_mixer(xn, w_in, v_gain, w_s, b_s, w_out):
    bn, length, _ = xn.shape
    t = min(length, CM_CHUNK)
    nc = length // t
    u, v = jnp.split(jax.nn.gelu(xn @ w_in), 2, axis=-1)
    v = layer_norm(v, v_gain)
    vg = v.reshape(bn, nc, t, CM_GROUPS, CM_GROUP_WIDTH)
    causal = jnp.tril(jnp.ones((t, t), dtype=w_s.dtype))
    ws = w_s[:, :t, :t] * causal
    mixed = jnp.einsum('gts,bcsgw->bctgw', ws, vg) + b_s[:, :t].T[None, None, :, :, None]
    out = (u * mixed.reshape(bn, length, CM_WIDTH)) @ w_out
    return out, v


def setup_inputs(seed: int = 0) -> dict:
    key = jax.random.key(seed)
    ks = iter(jax.random.split(key, 48))
    f32 = jnp.float32

    def nrm(shape, scale):
        return jax.random.normal(next(ks), shape, f32) * scale

    def gain(shape):
        return 1.0 + nrm(shape, 0.01)

    n_pages = PAST_LEN // PAGE_SIZE
    n_used = DEC_BATCH * n_pages
    n_pool = n_used + n_used // 4

    x_prompt = nrm((BATCH, SEQ, D_MODEL), 1.0)
    x_sample = nrm((DEC_BATCH, DEC_SEQ, D_MODEL), 1.0)
    state_ssm_re = nrm((N_LAYERS_A, DEC_BATCH, SSM_GROUPS, SSM_STATE), 0.5)
    state_ssm_im = nrm((N_LAYERS_A, DEC_BATCH, SSM_GROUPS, SSM_STATE), 0.5)
    cache_k = nrm((N_LAYERS_B, n_pool, PAGE_SIZE, SB_HEADS, SB_HEAD_DIM), 1.0)
    cache_v = nrm((N_LAYERS_B, n_pool, PAGE_SIZE, SB_HEADS, SB_HEAD_DIM), 1.0)
    page_table = jax.random.permutation(next(ks), n_pool)[:n_used].reshape(DEC_BATCH, n_pages).astype(jnp.int32)

    lam_shape = (N_LAYERS_A, SSM_GROUPS, SSM_STATE)
    ssm_lambda_re = -0.5 + nrm(lam_shape, 0.01)
    ssm_lambda_im = jnp.pi * jnp.arange(SSM_STATE, dtype=f32) + nrm(lam_shape, 0.01)
    ssm_log_dt = jax.random.uniform(next(ks), (N_LAYERS_A, SSM_GROUPS), f32,
                                    math.log(SSM_DT_MIN), math.log(SSM_DT_MAX))
    b_scale = (2.0 * SSM_GROUP_WIDTH) ** -0.5
    c_scale = SSM_STATE ** -0.5
    sb_logit_bias = jax.random.uniform(next(ks), (N_LAYERS_B, SB_HEADS), f32, SB_BIAS_MIN, SB_BIAS_MAX)

    return {
        'x_prompt': x_prompt,
        'x_sample': x_sample,
        'state_ssm_re': state_ssm_re,
        'state_ssm_im': state_ssm_im,
        'cache_k': cache_k,
        'cache_v': cache_v,
        'page_table': page_table,
        'norm_mix_pre': gain((DEPTH, D_MODEL)),
        'norm_mix_post': gain((DEPTH, D_MODEL)),
        'norm_ffn_pre': gain((DEPTH, D_MODEL)),
        'norm_ffn_post': gain((DEPTH, D_MODEL)),
        'w_ffn_in': nrm((DEPTH, D_MODEL, 2 * D_FF), D_MODEL ** -0.5),
        'w_ffn_out': nrm((DEPTH, D_FF, D_MODEL), D_FF ** -0.5),
        'w_ssm_in': nrm((N_LAYERS_A, D_MODEL, D_MODEL), D_MODEL ** -0.5),
        'ssm_lambda_re': ssm_lambda_re,
        'ssm_lambda_im': ssm_lambda_im,
        'ssm_log_dt': ssm_log_dt,
        'ssm_b_re': nrm((N_LAYERS_A, SSM_GROUPS, SSM_STATE, SSM_GROUP_WIDTH), b_scale),
        'ssm_b_im': nrm((N_LAYERS_A, SSM_GROUPS, SSM_STATE, SSM_GROUP_WIDTH), b_scale),
        'ssm_c_re': nrm((N_LAYERS_A, SSM_GROUPS, SSM_GROUP_WIDTH, SSM_STATE), c_scale),
        'ssm_c_im': nrm((N_LAYERS_A, SSM_GROUPS, SSM_GROUP_WIDTH, SSM_STATE), c_scale),
        'ssm_d': nrm((N_LAYERS_A, D_MODEL), 1.0),
        'w_ssm_glu': nrm((N_LAYERS_A, D_MODEL, 2 * D_MODEL), D_MODEL ** -0.5),
        'w_sb_qkv': nrm((N_LAYERS_B, D_MODEL, 3 * D_MODEL), D_MODEL ** -0.5),
        'w_sb_out': nrm((N_LAYERS_B, D_MODEL, D_MODEL), D_MODEL ** -0.5),
        'sb_logit_bias': sb_logit_bias,
        'w_cm_in': nrm((N_LAYERS_C, D_MODEL, 2 * CM_WIDTH), D_MODEL ** -0.5),
        'cm_v_norm': gain((N_LAYERS_C, CM_WIDTH)),
        'cm_w_s': nrm((N_LAYERS_C, CM_GROUPS, CM_CHUNK, CM_CHUNK), CM_CHUNK ** -0.5),
        'cm_b_s': gain((N_LAYERS_C, CM_GROUPS, CM_CHUNK)),
        'w_cm_out': nrm((N_LAYERS_C, CM_WIDTH, D_MODEL), CM_WIDTH ** -0.5),
    }


def reference(x_prompt, x_sample, state_ssm_re, state_ssm_im, cache_k, cache_v, page_table,
              norm_mix_pre, norm_mix_post, norm_ffn_pre, norm_ffn_post, w_ffn_in, w_ffn_out,
              w_ssm_in, ssm_lambda_re, ssm_lambda_im, ssm_log_dt, ssm_b_re, ssm_b_im, ssm_c_re, ssm_c_im,
              ssm_d, w_ssm_glu, w_sb_qkv, w_sb_out, sb_logit_bias, w_cm_in, cm_v_norm, cm_w_s, cm_b_s, w_cm_out):
    xp, xs = x_prompt, x_sample
    ssm_re_p, ssm_im_p, ssm_re_s, ssm_im_s = [], [], [], []
    k_p, v_p, k_s, v_s = [], [], [], []
    cm_v_s = []
    for i in range(DEPTH):
        kind, j = i % N_MIXERS, i // N_MIXERS
        xpn = rms_norm(xp, norm_mix_pre[i])
        xsn = rms_norm(xs, norm_mix_pre[i])
        if kind == 0:
            ssm_args = (w_ssm_in[j], ssm_lambda_re[j], ssm_lambda_im[j], ssm_log_dt[j], ssm_b_re[j],
                        ssm_b_im[j], ssm_c_re[j], ssm_c_im[j], ssm_d[j], w_ssm_glu[j])
            op, hr, hi = s5_mixer(xpn, *ssm_args)
            ssm_re_p.append(hr)
            ssm_im_p.append(hi)
            osmp, hr, hi = s5_mixer(xsn, *ssm_args, state_ssm_re[j], state_ssm_im[j])
            ssm_re_s.append(hr)
            ssm_im_s.append(hi)
        elif kind == 1:
            op, kk, vv = sb_mixer_prompt(xpn, w_sb_qkv[j], w_sb_out[j], sb_logit_bias[j])
            k_p.append(kk)
            v_p.append(vv)
            osmp, kk, vv = sb_mixer_sample(xsn, w_sb_qkv[j], w_sb_out[j], sb_logit_bias[j],
                                           cache_k[j], cache_v[j], page_table)
            k_s.append(kk)
            v_s.append(vv)
        else:
            cm_args = (w_cm_in[j], cm_v_norm[j], cm_w_s[j], cm_b_s[j], w_cm_out[j])
            op, _ = chunk_gmlp_mixer(xpn, *cm_args)
            osmp, vrows = chunk_gmlp_mixer(xsn, *cm_args)
            cm_v_s.append(vrows)
        xp = xp + rms_norm(op, norm_mix_post[i])
        xs = xs + rms_norm(osmp, norm_mix_post[i])
        xp = xp + rms_norm(swiglu_ffn(rms_norm(xp, norm_ffn_pre[i]), w_ffn_in[i], w_ffn_out[i]), norm_ffn_post[i])
        xs = xs + rms_norm(swiglu_ffn(rms_norm(xs, norm_ffn_pre[i]), w_ffn_in[i], w_ffn_out[i]), norm_ffn_post[i])
    ssm_re_prompt = jnp.stack(ssm_re_p)
    ssm_im_prompt = jnp.stack(ssm_im_p)
    ssm_re_sample = jnp.stack(ssm_re_s)
    ssm_im_sample = jnp.stack(ssm_im_s)
    k_prompt = jnp.stack(k_p)
    v_prompt = jnp.stack(v_p)
    k_sample = jnp.stack(k_s)
    v_sample = jnp.stack(v_s)
    cm_v_sample = jnp.stack(cm_v_s)
    return (xp, xs, ssm_re_prompt, ssm_im_prompt, ssm_re_sample, ssm_im_sample,
            k_prompt, v_prompt, k_sample, v_sample, cm_v_sample)
```

```python
import math
from contextlib import ExitStack
import numpy as np
import concourse.bass as bass
import concourse.mybir as mybir
from concourse.bass_utils import run_bass_kernel_spmd

F32 = mybir.dt.float32
BF16 = mybir.dt.bfloat16
I32 = mybir.dt.int32
ALU = mybir.AluOpType
AF = mybir.ActivationFunctionType
DSZ = {F32: 4, BF16: 2, I32: 4}

SEM_LIMIT = 30000
SAME_ENGINE_SYNC = True

D = 1024
KC = 8
DFF = 2816
HC = 22
NS = 128
EPS = 1e-6
TWO_PI = 2.0 * math.pi


class Buf:
    def __init__(self, name, tensor, nbytes, blk):
        self.name, self.t, self.nbytes, self.blk = name, tensor, nbytes, blk
        nb = max(1, (nbytes + blk - 1) // blk)
        self.w = [None] * nb
        self.r = [[] for _ in range(nb)]
        self.top = 0

    def view(self, off, n, dtype):
        esz = DSZ[dtype]
        assert off % 4 == 0 and (n * esz) % 4 == 0, (off, n)
        assert off + n * esz <= self.nbytes, (self.name, off, n, esz, self.nbytes)
        ap = self.t[:, off // 4:(off + n * esz) // 4]
        if dtype != F32:
            ap = ap.bitcast(dtype)
        return Ref(self, off, off + n * esz, ap, dtype)

    def alloc(self, n, dtype):
        esz = DSZ[dtype]
        nb = (n * esz + 31) // 32 * 32
        off = self.top
        self.top += nb
        assert self.top <= self.nbytes, ("arena overflow", self.name, self.top, self.nbytes)
        return self.view(off, n, dtype)

    def mark(self):
        return self.top

    def release(self, m):
        self.top = m


class Ref:
    def __init__(self, buf, lo, hi, ap, dtype):
        self.buf, self.lo, self.hi, self.ap, self.dtype = buf, lo, hi, ap, dtype

    def sub(self, a, b):
        esz = DSZ[self.dtype]
        return Ref(self.buf, self.lo + a * esz, self.lo + b * esz, self.ap[:, a:b], self.dtype)

    def v(self, fn):
        return Ref(self.buf, self.lo, self.hi, fn(self.ap), self.dtype)

    def re(self, pat, **kw):
        return Ref(self.buf, self.lo, self.hi, self.ap.rearrange(pat, **kw), self.dtype)

    def blocks(self):
        return range(self.lo // self.buf.blk, (self.hi - 1) // self.buf.blk + 1)


class Op:
    __slots__ = ("eng", "fn", "deps", "signal", "dma_key", "token", "batch")

    def __init__(self, eng, fn, dma_key=None):
        self.eng, self.fn, self.deps, self.signal = eng, fn, [], False
        self.dma_key, self.token, self.batch = dma_key, None, None


class Prog:
    ENGS = ("pe", "act", "dve", "pool", "sp")

    def __init__(self, nc):
        self.nc = nc
        self.ops = {e: [] for e in self.ENGS}
        self.stack = ExitStack()
        self.cur_batch = None
        self.out_dma_ops = []
        self.rot = {}

    def sbuf_arena(self, name, nbytes):
        t = self.stack.enter_context(self.nc.sbuf_tensor(name, [128, nbytes // 4], F32))
        return Buf(name, t, nbytes, 256)

    def psum_buf(self):
        t = self.stack.enter_context(self.nc.psum_tensor("ps", [128, 8 * 512], F32))
        return Buf("ps", t, 8 * 2048, 2048)

    def dram(self, name, shape, dtype, kind):
        return self.nc.dram_tensor(name, list(shape), dtype, kind=kind).ap()

    def _dep(self, op, prod):
        if prod is None or prod is op:
            return
        if prod.dma_key is None and prod.eng == op.eng and op.dma_key is None:
            if op.eng == "pe" or not SAME_ENGINE_SYNC:
                return
        prod.signal = True
        op.deps.append(prod)

    def op(self, eng, fn, reads=(), writes=(), dma_key=None, is_out=False):
        o = Op(eng, fn, dma_key)
        if dma_key is not None:
            o.signal = True
            o.batch = self.cur_batch
            if self.cur_batch is not None:
                self.cur_batch.append(o)
        for ref in reads:
            if ref is None:
                continue
            bf = ref.buf
            for k in ref.blocks():
                self._dep(o, bf.w[k])
                if bf.blk == 2048:
                    for rd in bf.r[k]:
                        if rd.eng != eng:
                            self._dep(o, rd)
        for ref in writes:
            bf = ref.buf
            for k in ref.blocks():
                self._dep(o, bf.w[k])
                for rd in bf.r[k]:
                    self._dep(o, rd)
        if len(o.deps) > 1:
            seen, dd = set(), []
            for d in o.deps:
                if id(d) not in seen:
                    seen.add(id(d))
                    dd.append(d)
            o.deps = dd
        for ref in reads:
            if ref is None:
                continue
            bf = ref.buf
            for k in ref.blocks():
                lst = bf.r[k]
                if o.dma_key is None:
                    lst[:] = [x for x in lst if not (x.dma_key is None and x.eng == o.eng)]
                lst.append(o)
        for ref in writes:
            bf = ref.buf
            for k in ref.blocks():
                bf.w[k] = o
                bf.r[k] = []
        self.ops[eng].append(o)
        if is_out:
            self.out_dma_ops.append(o)
        return o

    def finish(self):
        nc = self.nc
        fin = Op("sp", None)
        fin.deps = list(self.out_dma_ops)
        self.ops["sp"].append(fin)
        sems, counters = {}, {}

        def new_sem(key):
            n = len(sems)
            sems[n] = self.stack.enter_context(nc.semaphore("s%d" % n))
            counters[key] = [n, 0]

        for e in self.ENGS:
            for o in self.ops[e]:
                if not o.signal:
                    continue
                key = ("dma", o.dma_key) if o.dma_key is not None else ("eng", e)
                inc = 16 if o.dma_key is not None else 1
                if key not in counters or counters[key][1] + inc > SEM_LIMIT:
                    new_sem(key)
                c = counters[key]
                c[1] += inc
                o.token = (c[0], c[1])
        done = set()
        for e in self.ENGS:
            for o in self.ops[e]:
                if o.batch is not None and id(o.batch) not in done:
                    done.add(id(o.batch))
                    if len({m.token[0] for m in o.batch}) == 1:
                        mx = max(m.token[1] for m in o.batch)
                        for m in o.batch:
                            m.token = (m.token[0], mx)
        self.nsems = len(sems)
        stats = {}

        def emit(e, eng):
            seen, nwait = {}, 0
            for o in self.ops[e]:
                for d in o.deps:
                    sid, val = d.token
                    if seen.get(sid, 0) < val:
                        eng.wait_ge(sems[sid], val)
                        seen[sid] = val
                        nwait += 1
                if o.fn is None:
                    continue
                ins = o.fn(eng)
                if o.signal:
                    ins.then_inc(sems[o.token[0]], 16 if o.dma_key is not None else 1)
            stats[e] = (len(self.ops[e]), nwait)

        with nc.Block() as block:
            @block.tensor
            def _(eng):
                emit("pe", eng)

            @block.scalar
            def _(eng):
                emit("act", eng)

            @block.vector
            def _(eng):
                emit("dve", eng)

            @block.gpsimd
            def _(eng):
                emit("pool", eng)

            @block.sync
            def _(eng):
                emit("sp", eng)
        self.stats = stats
        self.stack.close()


class StopBuild(Exception):
    pass


class K:
    def __init__(self, NT, layers, n_cores=8, with_sample=True, with_attn_sample=True, pool_pages=2560):
        self.NT, self.layers = NT, layers
        self.pool_pages = pool_pages
        self.stop = 0
        self.with_sample = with_sample
        self.with_attn_sample = with_attn_sample
        self.n_cores = n_cores
        self.nc = bass.Bass("TRN2", target_bir_lowering=False)
        self.P = Prog(self.nc)
        P = self.P
        self.PS = P.psum_buf()
        self.PERS = P.sbuf_arena("pers", 4 * (KC * NT) + 4 * KC * NS + 2 * 1024)
        self.AR = P.sbuf_arena("arena", 204 * 1024 - self.PERS.nbytes - 1024)
        self.X = self.PERS.alloc(KC * NT, F32)
        self.XS = self.PERS.alloc(KC * NS, F32)
        self.bank_rot = {}
        self.inputs = {}
        self.outputs = {}

    def din(self, name, shape, dtype=F32):
        ap = self.P.dram(name, shape, dtype, "ExternalInput")
        self.inputs[name] = ap
        return ap

    def dout(self, name, shape, dtype=F32):
        ap = self.P.dram(name, shape, dtype, "ExternalOutput")
        self.outputs[name] = ap
        return ap

    def bank(self, i, n=512, off=0):
        r = self.PS.view(i * 2048, 512, F32)
        return r.sub(off, off + n)

    def nb(self, cls, choices):
        k = self.bank_rot.get(cls, 0)
        self.bank_rot[cls] = k + 1
        return choices[k % len(choices)]

    def mm(self, out, lhsT, rhs, start=True, stop=True):
        self.P.op("pe", lambda e: e.matmul(out.ap, lhsT.ap, rhs.ap, start=start, stop=stop),
                  reads=[lhsT, rhs], writes=[out])

    def tr(self, out, in_, ident):
        self.P.op("pe", lambda e: e.transpose(out.ap, in_.ap, ident.ap), reads=[in_, ident], writes=[out])

    def act(self, out, in_, func, bias=None, scale=None, eng="act", accum=None):
        kw = {}
        rd = [in_]
        wr = [out]
        if accum is not None:
            kw["accum_out"] = accum.ap
            wr.append(accum)
        if bias is not None:
            if isinstance(bias, Ref):
                kw["bias"] = bias.ap
                rd.append(bias)
            else:
                kw["bias"] = bias
        if scale is not None:
            if isinstance(scale, Ref):
                kw["scale"] = scale.ap
                rd.append(scale)
            else:
                kw["scale"] = scale
        self.P.op("act", lambda e: e.activation(out.ap, in_.ap, func, **kw), reads=rd, writes=wr)

    def cp(self, eng, out, in_):
        if eng == "act":
            self.P.op("act", lambda e: e.copy(out.ap, in_.ap), reads=[in_], writes=[out])
        else:
            self.P.op(eng, lambda e: e.tensor_copy(out.ap, in_.ap), reads=[in_], writes=[out])

    def tt(self, eng, out, a, b, op):
        self.P.op(eng, lambda e: e.tensor_tensor(out.ap, a.ap, b.ap, op), reads=[a, b], writes=[out])

    def ts(self, eng, out, a, s1, s2, op0, op1=None):
        rd = [a]
        v1 = s1.ap if isinstance(s1, Ref) else s1
        v2 = s2.ap if isinstance(s2, Ref) else s2
        if isinstance(s1, Ref):
            rd.append(s1)
        if isinstance(s2, Ref):
            rd.append(s2)
        if op1 is None:
            self.P.op(eng, lambda e: e.tensor_scalar(out.ap, a.ap, v1, None, op0), reads=rd, writes=[out])
        else:
            self.P.op(eng, lambda e: e.tensor_scalar(out.ap, a.ap, v1, v2, op0, op1), reads=rd, writes=[out])

    def stt(self, out, in0, scalar, in1, op0, op1):
        rd = [in0, in1]
        sv = scalar.ap if isinstance(scalar, Ref) else scalar
        if isinstance(scalar, Ref):
            rd.append(scalar)
        self.P.op("dve", lambda e: e.scalar_tensor_tensor(out.ap, in0.ap, sv, in1.ap, op0, op1),
                  reads=rd, writes=[out])

    def memset(self, eng, out, val):
        self.P.op(eng, lambda e: e.memset(out.ap, val), writes=[out])

    def dma(self, eng, out, in_, key, reads=(), writes=(), is_out=False, slow=False):
        oa = out.ap if isinstance(out, Ref) else out
        ia = in_.ap if isinstance(in_, Ref) else in_
        rd = list(reads) + ([in_] if isinstance(in_, Ref) else [])
        wr = list(writes) + ([out] if isinstance(out, Ref) else [])
        if slow:
            fn = lambda e: e.dma_start(out=oa, in_=ia, allow_slow_non_contiguous=True)
        else:
            fn = lambda e: e.dma_start(out=oa, in_=ia)
        return self.P.op(eng, fn, reads=rd, writes=wr, dma_key=key, is_out=is_out)


def _declare_io(self):
    NT = self.NT
    d = self.din
    self.xp = d("xp", [NT, D])
    self.xs = d("xs", [NS, D])
    self.st_re = d("st_re", [2, NS, 4096])
    self.st_im = d("st_im", [2, NS, 4096])
    self.kpool = d("kpool", [self.pool_pages * 1024, 128])
    self.vpool = d("vpool", [self.pool_pages * 1024, 128])
    self.ptab = d("ptab", [NS, 16], I32)
    self.n_mix_pre = d("n_mix_pre", [4, D])
    self.n_mix_post = d("n_mix_post", [4, D])
    self.n_ffn_pre = d("n_ffn_pre", [4, D])
    self.n_ffn_post = d("n_ffn_post", [4, D])
    self.w_ffn_in = d("w_ffn_in", [4, D, 2 * DFF])
    self.w_ffn_out = d("w_ffn_out", [4, DFF, D])
    self.w_ssm_in = d("w_ssm_in", [2, D, D])
    self.lam_re = d("lam_re", [2, 64, 64])
    self.lam_im = d("lam_im", [2, 64, 64])
    self.log_dt = d("log_dt", [2, 64])
    self.b_re = d("b_re", [2, 64, 64, 16])
    self.b_im = d("b_im", [2, 64, 64, 16])
    self.c_re = d("c_re", [2, 64, 16, 64])
    self.c_im = d("c_im", [2, 64, 16, 64])
    self.ssm_d = d("ssm_d", [2, D])
    self.w_glu = d("w_glu", [2, D, 2 * D])
    self.w_qkv = d("w_qkv", [1, D, 3 * D])
    self.w_sbo = d("w_sbo", [1, D, D])
    self.sb_bias = d("sb_bias", [1, 16])
    self.w_cm_in = d("w_cm_in", [1, D, 2 * D])
    self.cm_vn = d("cm_vn", [1, D])
    self.cm_ws = d("cm_ws", [1, 8, 128, 128])
    self.cm_bs = d("cm_bs", [1, 8, 128])
    self.w_cm_out = d("w_cm_out", [1, D, D])
    o = self.dout
    self.yp = o("yp", [NT, D])
    self.ys = o("ys", [NS, D])
    self.o_re_p = o("o_re_p", [2, 32, 128])
    self.o_im_p = o("o_im_p", [2, 32, 128])
    self.o_re_s = o("o_re_s", [2, NS, 4096])
    self.o_im_s = o("o_im_s", [2, NS, 4096])
    self.o_kp = o("o_kp", [NT, D])
    self.o_vp = o("o_vp", [NT, D])
    self.o_ks = o("o_ks", [NS, D])
    self.o_vs = o("o_vs", [NS, D])
    self.o_cmv = o("o_cmv", [NS, D])
    if self.n_cores == 1:
        self.o_dbg = o("o_dbg", [128, NS])


def _setup_consts(self):
    PE = self.PERS
    self.ONESB = PE.alloc(128, BF16)
    self.IDENT = PE.alloc(128, F32)
    self.IDENTB = PE.alloc(128, BF16)
    self.EPSB = PE.alloc(8, F32)
    self.G = PE.alloc(KC * 16, F32)
    self.memset("dve", self.ONESB, 1.0)
    self.memset("dve", self.EPSB, EPS)
    m = self.AR.mark()
    IOT = self.AR.alloc(128, F32)
    self.P.op("pool", lambda e: e.iota(IOT.ap, [[1, 128]], base=0, channel_multiplier=-1,
                                       allow_small_or_imprecise_dtypes=True), writes=[IOT])
    self.ts("dve", self.IDENT, IOT, 0.0, None, ALU.is_equal)
    self.cp("dve", self.IDENTB, self.IDENT)
    self.HM = PE.alloc(8, F32)
    PIX = self.AR.alloc(8, F32)
    self.P.op("pool", lambda e: e.iota(PIX.ap, [[0, 8]], base=0, channel_multiplier=1,
                                       allow_small_or_imprecise_dtypes=True), writes=[PIX])
    self.ts("dve", self.HM.sub(0, 1), PIX.sub(0, 1), 64.0, None, ALU.is_lt)
    self.ts("dve", self.HM.sub(1, 2), PIX.sub(0, 1), 64.0, None, ALU.is_ge)
    GS = self.AR.alloc(D, F32)
    for kind, src in enumerate((self.n_mix_pre, self.n_mix_post, self.n_ffn_pre, self.n_ffn_post)):
        dst = GS.v(lambda ap, kind=kind: ap[4 * kind:4 * kind + 4, :])
        self.dma("sp", dst, src, "gload")
    for c in range(KC):
        pb = self.bank(6, 16)
        self.tr(pb, GS.v(lambda ap, c=c: ap[0:16, c * 128:(c + 1) * 128]),
                self.IDENT.v(lambda ap: ap[0:16, 0:16]))
        self.cp("act", self.G.sub(c * 16, (c + 1) * 16), pb)
    self.AR.release(m)


def ckpt(self, n):
    if self.stop == n:
        raise StopBuild()


def gcol(self, kind, l, c):
    v = 4 * kind + l
    return self.G.sub(c * 16 + v, c * 16 + v + 1)


def tiles(self):
    NT = self.NT
    res = []
    for t0 in range(0, NT, 512):
        res.append((lambda c, t0=t0: self.X.sub(c * NT + t0, c * NT + t0 + 512), 512, ("p", t0)))
    if self.with_sample:
        res.append((lambda c: self.XS.sub(c * NS, (c + 1) * NS), NS, ("s", 0)))
    return res


def _load_x(self):
    NT = self.NT
    m = self.AR.mark()
    ST = [self.AR.alloc(D, F32) for _ in range(2)]
    jobs = [(self.xp, tt, self.X, NT) for tt in range(NT // 128)]
    if self.with_sample:
        jobs.append((self.xs, 0, self.XS, NS))
    for i, (src, tt, dstbuf, n) in enumerate(jobs):
        st = ST[i % 2]
        self.dma("sp", st, src[tt * 128:(tt + 1) * 128, :], "xload%d" % (i % 2))
        for half in range(2):
            pb = self.bank(self.nb("tr", [4, 5]))
            for q in range(4):
                c = half * 4 + q
                self.tr(pb.sub(q * 128, (q + 1) * 128), st.sub(c * 128, (c + 1) * 128), self.IDENT)
            dst = dstbuf.v(lambda ap, half=half, tt=tt, n=n: ap.rearrange("p (c t) -> p c t", c=KC)
                           [:, half * 4:half * 4 + 4, tt * 128:(tt + 1) * 128])
            self.cp("act", dst, pb.re("p (q t) -> p q t", q=4))
    self.AR.release(m)


def _store_y(self):
    NT = self.NT
    m = self.AR.mark()
    ST = [self.AR.alloc(D, F32) for _ in range(2)]
    jobs = [(self.yp, tt, self.X, NT) for tt in range(NT // 128)]
    if self.with_sample:
        jobs.append((self.ys, 0, self.XS, NS))
    for i, (dst, tt, srcbuf, n) in enumerate(jobs):
        st = ST[i % 2]
        for half in range(2):
            pb = self.bank(self.nb("tr", [4, 5]))
            for q in range(4):
                c = half * 4 + q
                self.tr(pb.sub(q * 128, (q + 1) * 128), srcbuf.sub(c * n + tt * 128, c * n + (tt + 1) * 128), self.IDENT)
            self.cp("act" if half == 0 else "dve", st.sub(half * 512, (half + 1) * 512), pb)
        self.dma("sp", dst[tt * 128:(tt + 1) * 128, :], st, "ystore%d" % (i % 2), is_out=True)
    self.AR.release(m)


def rms_stats(self, src, n, sc):
    pss = self.bank(7, n)
    for c in range(KC):
        sq = sc["SQ"][c % 2].sub(0, n)
        s = src(c)
        if c % 2 == 0:
            self.act(sq, s, AF.Square)
        else:
            self.tt("pool", sq, s, s, ALU.mult)
        self.mm(pss, self.ONESB, sq, start=(c == 0), stop=(c == KC - 1))
    ln = sc["LNV"].sub(0, n)
    rs = sc["RSTD"].sub(0, n)
    self.act(ln, pss, AF.Ln, bias=self.EPSB.sub(0, 1), scale=1.0 / D)
    self.act(rs, ln, AF.Exp, scale=-0.5)
    return rs


def norm_to(self, src, n, kind, l, dst, sc):
    rs = self.rms_stats(src, n, sc)
    for c in range(KC):
        self.stt(dst(c), src(c), self.gcol(kind, l, c), rs, ALU.mult, ALU.mult)


def norm_add(self, src, n, kind, l, xfn, sc):
    rs = self.rms_stats(src, n, sc)
    for c in range(KC):
        tmp = sc["TMP"][c % 2].sub(0, n)
        self.stt(tmp, src(c), self.gcol(kind, l, c), rs, ALU.mult, ALU.mult)
        xd = xfn(c)
        self.tt("pool" if c % 2 else "dve", xd, xd, tmp, ALU.add)


def alloc_norm_scratch(self):
    AR = self.AR
    return dict(SQ=[AR.alloc(512, BF16) for _ in range(2)], LNV=AR.alloc(512, F32), RSTD=AR.alloc(512, F32),
                TMP=[AR.alloc(512, F32) for _ in range(2)])


def wslab(self, dst, src_ap, key):
    return self.dma("pool", dst, src_ap, key)


def linear_fm(self, w2d, col0, noc, rhs, n, consume, slabs, key, banks=(0, 1, 2, 3), kc_n=KC):
    wv = w2d.rearrange("(k p) n -> p k n", p=128)
    for sp in range(0, noc, 2):
        nh = min(2, noc - sp)
        si = self.nb(key, [0, 1])
        slab = slabs[si]
        dst = slab.v(lambda ap, nh=nh: ap.rearrange("p (k n) -> p k n", k=kc_n)[:, :, 0:nh * 128])
        self.wslab(dst, wv[:, :, col0 + sp * 128: col0 + (sp + nh) * 128], "%s%d" % (key, si))
        for j in range(nh):
            oc = sp + j
            pb = self.bank(self.nb("lin", list(banks)), n)
            for k in range(kc_n):
                l = slab.v(lambda ap, k=k, j=j: ap[:, k * 256 + j * 128: k * 256 + (j + 1) * 128])
                self.mm(pb, l, rhs(k), start=(k == 0), stop=(k == kc_n - 1))
            consume(oc, pb)


def ffn(self, l):
    AR = self.AR
    m = AR.mark()
    WOUT = AR.alloc(HC * D, BF16)
    for hc in range(HC):
        self.wslab(WOUT.sub(hc * D, (hc + 1) * D), self.w_ffn_out[l, hc * 128:(hc + 1) * 128, :], "wout")
    WIN = [AR.alloc(KC * 512, BF16) for _ in range(2)]
    XN = AR.alloc(KC * 512, BF16)
    H = AR.alloc(HC * 512, BF16)
    O = AR.alloc(KC * 512, F32)
    SG = [AR.alloc(512, F32) for _ in range(2)]
    sc = self.alloc_norm_scratch()
    wv = self.w_ffn_in[l].rearrange("(k p) n -> p k n", p=128)
    for (xfn, n, tag) in self.tiles():
        self.norm_to(xfn, n, 2, l, lambda c, n=n: XN.sub(c * 512, c * 512 + n), sc)
        for hp in range(0, HC, 2):
            si = self.nb("ffnw", [0, 1])
            slab = WIN[si]
            for part, off in ((0, 0), (1, DFF)):
                dst = slab.v(lambda ap, part=part: ap.rearrange("p (k n) -> p k n", k=KC)[:, :, part * 256:part * 256 + 256])
                self.wslab(dst, wv[:, :, off + hp * 128: off + (hp + 2) * 128], "ffnw%d" % si)
            for j in range(2):
                hc = hp + j
                pg = self.bank(self.nb("ffng", [0, 2]), n)
                pu = self.bank(self.nb("ffnu", [1, 3]), n)
                for k in range(KC):
                    lg = slab.v(lambda ap, k=k, j=j: ap[:, k * 512 + j * 128: k * 512 + (j + 1) * 128])
                    self.mm(pg, lg, XN.sub(k * 512, k * 512 + n), start=(k == 0), stop=(k == KC - 1))
                for k in range(KC):
                    lu = slab.v(lambda ap, k=k, j=j: ap[:, k * 512 + 256 + j * 128: k * 512 + 256 + (j + 1) * 128])
                    self.mm(pu, lu, XN.sub(k * 512, k * 512 + n), start=(k == 0), stop=(k == KC - 1))
                sg = SG[hc % 2].sub(0, n)
                self.act(sg, pg, AF.Silu)
                self.tt("dve", H.sub(hc * 512, hc * 512 + n), sg, pu, ALU.mult)
        for oc in range(KC):
            po = self.bank(self.nb("ffno", [4, 5]), n)
            for hc in range(HC):
                self.mm(po, WOUT.sub(hc * D + oc * 128, hc * D + (oc + 1) * 128), H.sub(hc * 512, hc * 512 + n),
                        start=(hc == 0), stop=(hc == HC - 1))
            self.cp("act", O.sub(oc * 512, oc * 512 + n), po)
        self.norm_add(lambda c, n=n: O.sub(c * 512, c * 512 + n), n, 3, l, xfn, sc)
    AR.release(m)


for _f in (ckpt, _declare_io, _setup_consts, gcol, tiles, _load_x, _store_y, rms_stats, norm_to, norm_add,
           alloc_norm_scratch, wslab, linear_fm, ffn):
    setattr(K, _f.__name__, _f)


def bc(ap, shape):
    return ap.broadcast_to(list(shape))


def s5_layer(self, l, j):
    NT = self.NT
    NJ = NT // 8
    LOGJ = int(math.log2(NJ))
    AR, P = self.AR, self.P
    m0 = AR.mark()
    PI = math.pi
    def pl(n=32):
        return AR.alloc(n, F32)
    LR, LI, LDT = pl(), pl(), pl()
    STG = AR.alloc(128, F32)
    STG2 = AR.alloc(128, F32)

    def load_pl(dst, src32x128, stg):
        self.dma("sp", stg.v(lambda ap: ap[0:32, :]), src32x128, "plload")
        pb = self.bank(6, 32)
        self.tr(pb, stg.v(lambda ap: ap[0:32, :]), self.IDENT.v(lambda ap: ap[0:32, 0:32]))
        self.cp("act", dst, pb)

    load_pl(LR, self.lam_re[j].rearrange("(b g) n -> b (g n)", g=2), STG)
    load_pl(LI, self.lam_im[j].rearrange("(b g) n -> b (g n)", g=2), STG2)
    LD2 = AR.alloc(2, F32)
    self.dma("sp", LD2.v(lambda ap: ap[0:32, :]), self.log_dt[j].rearrange("(b g) -> b g", g=2), "plload")
    STG3 = AR.alloc(128, F32)
    for g2 in range(2):
        self.cp("dve", STG3.v(lambda ap, g2=g2: ap[0:32, g2 * 64:(g2 + 1) * 64]),
                LD2.v(lambda ap, g2=g2: bc(ap[0:32, g2:g2 + 1], [32, 64])))
    pb = self.bank(6, 32)
    self.tr(pb, STG3.v(lambda ap: ap[0:32, :]), self.IDENT.v(lambda ap: ap[0:32, 0:32]))
    self.cp("act", LDT, pb)
    DT, LRD, TH, RHO, T1, T2, MSK = pl(), pl(), pl(), pl(), pl(), pl(), pl()
    SINT, COST, ARE, AIM = pl(), pl(), pl(), pl()
    self.act(DT, LDT, AF.Exp)
    self.tt("dve", LRD, LR, DT, ALU.mult)
    self.tt("dve", TH, LI, DT, ALU.mult)
    self.act(RHO, LRD, AF.Exp)

    def wrap_pos(r, times):
        for _ in range(times):
            self.ts("dve", MSK, r, PI, None, ALU.is_gt)
            self.stt(r, MSK, -TWO_PI, r, ALU.mult, ALU.add)

    def wrap_neg(r):
        self.ts("dve", MSK, r, -PI, None, ALU.is_lt)
        self.stt(r, MSK, TWO_PI, r, ALU.mult, ALU.add)

    wrap_pos(TH, 4)
    wrap_neg(TH)
    self.act(SINT, TH, AF.Sin)
    self.ts("dve", T1, TH, PI / 2, None, ALU.add)
    wrap_pos(T1, 1)
    self.act(COST, T1, AF.Sin)
    self.tt("dve", ARE, RHO, COST, ALU.mult)
    self.tt("dve", AIM, RHO, SINT, ALU.mult)
    DEN, QR, QI, AM1 = pl(), pl(), pl(), pl()
    self.tt("dve", DEN, LR, LR, ALU.mult)
    self.tt("dve", T1, LI, LI, ALU.mult)
    self.tt("dve", DEN, DEN, T1, ALU.add)
    self.P.op("dve", lambda e: e.reciprocal(DEN.ap, DEN.ap), reads=[DEN], writes=[DEN])
    self.ts("dve", AM1, ARE, -1.0, None, ALU.add)
    self.tt("dve", T1, AM1, LR, ALU.mult)
    self.tt("dve", T2, AIM, LI, ALU.mult)
    self.tt("dve", T1, T1, T2, ALU.add)
    self.tt("dve", QR, T1, DEN, ALU.mult)
    self.tt("dve", T1, AIM, LR, ALU.mult)
    self.tt("dve", T2, AM1, LI, ALU.mult)
    self.tt("dve", T1, T1, T2, ALU.subtract)
    self.tt("dve", QI, T1, DEN, ALU.mult)
    PWR = AR.alloc(32 * 9, F32)
    PWI = AR.alloc(32 * 9, F32)
    pw = lambda T, k: T.v(lambda ap, k=k: ap.rearrange("p (b k) -> p b k", k=9)[:, :, k])
    self.memset("dve", pw(PWR, 0), 1.0)
    self.memset("dve", pw(PWI, 0), 0.0)
    self.cp("dve", pw(PWR, 1), ARE)
    self.cp("dve", pw(PWI, 1), AIM)
    for k in range(2, 9):
        self.tt("dve", T1, pw(PWR, k - 1), ARE, ALU.mult)
        self.tt("dve", T2, pw(PWI, k - 1), AIM, ALU.mult)
        self.tt("dve", pw(PWR, k), T1, T2, ALU.subtract)
        self.tt("dve", T1, pw(PWR, k - 1), AIM, ALU.mult)
        self.tt("dve", T2, pw(PWI, k - 1), ARE, ALU.mult)
        self.tt("dve", pw(PWI, k), T1, T2, ALU.add)
    RHO8, IA8R, IA8I, E1R, E1I = pl(), pl(), pl(), pl(), pl()
    self.act(RHO8, LRD, AF.Exp, scale=8.0)
    self.act(T1, LRD, AF.Exp, scale=-16.0)
    self.tt("dve", IA8R, pw(PWR, 8), T1, ALU.mult)
    self.stt(IA8I, pw(PWI, 8), -1.0, T1, ALU.mult, ALU.mult)
    self.act(T2, LRD, AF.Exp, scale=-8.0)
    self.tt("dve", E1R, pw(PWR, 8), T2, ALU.mult)
    self.tt("dve", E1I, pw(PWI, 8), T2, ALU.mult)
    self.ckpt(1)
    BBR = AR.alloc(512, F32)
    BBI = AR.alloc(512, F32)
    mB = AR.mark()
    BR = AR.alloc(512, F32)
    BI = AR.alloc(512, F32)
    TB1 = AR.alloc(512, F32)
    TB2 = AR.alloc(512, F32)
    bsrc = lambda t: t[j].rearrange("(b g) n q -> (g n) b q", g=2)
    self.dma("sp", BR.re("p (b q) -> p b q", q=16), bsrc(self.b_re), "bload")
    self.dma("sp", BI.re("p (b q) -> p b q", q=16), bsrc(self.b_im), "bload")
    q3 = lambda T: T.v(lambda ap: bc(ap.unsqueeze(2), [128, 32, 16]))
    b3 = lambda T: T.re("p (b q) -> p b q", q=16)
    self.tt("dve", b3(TB1), b3(BR), q3(QR), ALU.mult)
    self.tt("pool", b3(TB2), b3(BI), q3(QI), ALU.mult)
    self.tt("dve", BBR, TB1, TB2, ALU.subtract)
    self.tt("dve", b3(TB1), b3(BI), q3(QR), ALU.mult)
    self.tt("pool", b3(TB2), b3(BR), q3(QI), ALU.mult)
    self.tt("dve", BBI, TB1, TB2, ALU.add)
    AR.release(mB)
    self.ckpt(2)
    CR = AR.alloc(512, F32)
    CI = AR.alloc(512, F32)
    mC = AR.mark()
    CP = AR.alloc(KC * 128, F32)
    for (dstC, srcC) in ((CR, self.c_re), (CI, self.c_im)):
        self.memset("dve", CP, 0.0)
        self.dma("sp", CP.v(lambda ap: ap.rearrange("p (c n) -> p c n", c=KC)[:, :, 64:128]),
                 srcC[j].rearrange("(c g) q n -> (g q) c n", g=8), "cload")
        for c in range(KC):
            pa = self.bank(self.nb("tr", [4, 5]), 256)
            self.tr(pa.v(lambda ap: ap[0:64, 0:128]), CP.sub(c * 128 + 64, c * 128 + 128), self.IDENT)
            self.tr(pa.sub(128, 256), CP.sub(c * 128, c * 128 + 128), self.IDENT)
            d0 = dstC.v(lambda ap, c=c: ap.rearrange("p (b q) -> p b q", q=16)[0:64, 4 * c:4 * c + 4, :])
            s0 = pa.v(lambda ap: ap[0:64, 0:128].rearrange("p (b g q) -> p b g q", g=2, q=16)[:, :, 0, :])
            self.cp("act", d0, s0)
            d1 = dstC.v(lambda ap, c=c: ap.rearrange("p (b q) -> p b q", q=16)[64:128, 4 * c:4 * c + 4, :])
            s1 = pa.v(lambda ap: ap[64:128, 128:256].rearrange("p (b g q) -> p b g q", g=2, q=16)[:, :, 1, :])
            self.cp("act", d1, s1)
    AR.release(mC)
    self.ckpt(3)
    DCOL = AR.alloc(64, F32)
    DQ = AR.alloc(64, F32)
    self.dma("sp", DQ.v(lambda ap: ap[0:16, :]), self.ssm_d[j].rearrange("(g q) -> q g", q=16), "dload", slow=True)
    pb = self.bank(6, 64)
    REP = AR.alloc(128, F32)
    self.cp("dve", REP.v(lambda ap: ap[0:16, :].rearrange("p (r q) -> p r q", r=8)),
            self.IDENT.v(lambda ap: bc(ap[0:16, 0:16].unsqueeze(1), [16, 8, 16])))
    self.mm(pb, REP.v(lambda ap: ap[0:16, :]), DQ.v(lambda ap: ap[0:16, :]))
    self.cp("act", DCOL, pb)
    self.ckpt(4)
    MM = AR.alloc(64 * 128, BF16)
    Mab = lambda a, b: MM.sub((a * 8 + b) * 128, (a * 8 + b + 1) * 128)
    MASKT = AR.alloc(128, F32)
    IDX = AR.alloc(128, F32)
    mM = AR.mark()
    PMC = AR.alloc(128, F32)
    RM = AR.alloc(8, F32)
    DK = AR.alloc(128, F32)
    PIDX = AR.alloc(8, F32)
    P.op("pool", lambda e: e.iota(PMC.ap, [[-1, 128]], base=0, channel_multiplier=1,
                                  allow_small_or_imprecise_dtypes=True), writes=[PMC])
    P.op("pool", lambda e: e.iota(PIDX.ap, [[-16, 8]], base=0, channel_multiplier=1,
                                  allow_small_or_imprecise_dtypes=True), writes=[PIDX])
    self.ts("dve", RM, PIDX, 0.0, None, ALU.is_ge)
    self.ts("dve", PIDX, PIDX, 16.0, None, ALU.is_lt)
    self.tt("dve", RM, RM, PIDX, ALU.mult)
    for kk in range(-7, 8):
        self.ts("dve", DK, PMC, 16.0 * kk, None, ALU.is_equal)
        for a in range(8):
            b = a - kk
            if 0 <= b < 8:
                self.ts("dve" if a % 2 else "pool", Mab(a, b), DK, RM.sub(a, a + 1), None, ALU.mult)
    RP = AR.alloc(8, F32)
    TI = AR.alloc(128, F32)
    P.op("pool", lambda e: e.iota(TI.ap, [[1, 8], [0, 16]], base=0, channel_multiplier=0,
                                  allow_small_or_imprecise_dtypes=True), writes=[TI])
    P.op("pool", lambda e: e.iota(RP.ap, [[1, 8]], base=0, channel_multiplier=0,
                                  allow_small_or_imprecise_dtypes=True), writes=[RP])
    RPC = AR.alloc(8, F32)
    self.tt("dve", RPC, RP, RM, ALU.mult)
    RPS = AR.alloc(8, F32)
    P.op("dve", lambda e: e.reduce_sum(RPS.sub(0, 1).ap, RPC.ap, axis=mybir.AxisListType.X), reads=[RPC], writes=[RPS])
    self.ts("dve", MASKT, TI, RPS.sub(0, 1), 7.0, ALU.add, ALU.is_ge)
    self.memset("dve", IDX, 0.0)
    for t in range(8):
        self.tt("dve", IDX, IDX, Mab(7 - t, t), ALU.add)
    AR.release(mM)
    self.ckpt(5)
    UTD = AR.alloc(KC * NT, BF16)
    US = AR.alloc(KC * NS, BF16)
    ZS = AR.alloc(KC * NS, BF16)
    FINR, FINI = pl(), pl()
    mU = AR.mark()
    XN = AR.alloc(KC * 512, BF16)
    SL = [AR.alloc(KC * 256, BF16) for _ in range(2)]
    sc = self.alloc_norm_scratch()
    for (xfn, n, tag) in self.tiles():
        self.norm_to(xfn, n, 0, l, lambda c, n=n: XN.sub(c * 512, c * 512 + n), sc)
        if tag[0] == "p":
            t0 = tag[1]
            wv = self.w_ssm_in[j].rearrange("(k p) n -> p k n", p=128)
            for sp in range(0, KC, 2):
                si = self.nb("s5w", [0, 1])
                slab = SL[si]
                self.wslab(slab.re("p (k n) -> p k n", k=KC), wv[:, :, sp * 128:(sp + 2) * 128], "s5w%d" % si)
                for jj in range(2):
                    oc = sp + jj
                    pbk = self.bank(self.nb("lin", [0, 1, 2, 3]), 512)
                    for r in range(8):
                        for k in range(KC):
                            lw = slab.v(lambda ap, k=k, jj=jj: ap[:, k * 256 + jj * 128:k * 256 + (jj + 1) * 128])
                            rr = XN.v(lambda ap, k=k, r=r: ap.rearrange("p (k j r) -> p k r j", k=KC, r=8)[:, k, r, :])
                            self.mm(pbk.sub(r * 64, (r + 1) * 64), lw, rr, start=(k == 0), stop=(k == KC - 1))
                    dst = UTD.v(lambda ap, oc=oc, t0=t0: ap.rearrange("p (c r j) -> p c r j", c=KC, r=8)
                                [:, oc, :, t0 // 8:t0 // 8 + 64])
                    self.cp("act", dst, pbk.re("p (r j) -> p r j", r=8))
        else:
            self.linear_fm(self.w_ssm_in[j], 0, KC, lambda k: XN.sub(k * 512, k * 512 + NS), NS,
                           lambda oc, pb: self.cp("act", US.sub(oc * NS, (oc + 1) * NS), pb), SL, "s5w")
    AR.release(mU)
    self.ckpt(6)
    mP = AR.mark()
    WINR = [AR.alloc(4 * 128, BF16) for _ in range(2)]
    WINI = [AR.alloc(4 * 128, BF16) for _ in range(2)]
    WOBR = [AR.alloc(4 * 128, BF16) for _ in range(2)]
    WOBI = [AR.alloc(4 * 128, BF16) for _ in range(2)]
    KT = AR.alloc(8 * 128, BF16)
    UP = AR.alloc(8 * NJ, BF16)
    HBR = AR.alloc(4 * NJ, BF16)
    HBI = AR.alloc(4 * NJ, BF16)
    YH = AR.alloc(8 * NJ, BF16)
    YL = AR.alloc(8 * NJ, BF16)
    if self.with_sample:
        UPS0 = AR.alloc(8 * NS, BF16)
        UPS7 = AR.alloc(8 * NS, BF16)
        H0R = AR.alloc(4 * NS, F32)
        H0I = AR.alloc(4 * NS, F32)
        H0BR = AR.alloc(4 * NS, BF16)
        H0BI = AR.alloc(4 * NS, BF16)
        HNR = AR.alloc(4 * NS, F32)
        HNI = AR.alloc(4 * NS, F32)
        YHS = AR.alloc(8 * NS, BF16)
        YLS = AR.alloc(8 * NS, BF16)
    Z = UTD
    for c in range(KC):
        b0 = 4 * c
        mT = AR.mark()
        WTR = AR.alloc(512, F32)
        WTI = AR.alloc(512, F32)
        PSR = AR.alloc(512, F32)
        PSI_ = AR.alloc(512, F32)
        WOR = AR.alloc(512, F32)
        WOI = AR.alloc(512, F32)
        TA = AR.alloc(512, F32)
        TBb = AR.alloc(512, F32)
        v4 = lambda T: T.re("p (b r q) -> p b r q", b=4, r=8)
        pwA = lambda T, k0: T.v(lambda ap, k0=k0: bc(ap.rearrange("p (b k) -> p b k", k=9)[:, b0:b0 + 4, k0:k0 + 8].unsqueeze(3), [128, 4, 8, 16]))
        bq = lambda T: T.v(lambda ap: bc(ap.rearrange("p (b q) -> p b q", q=16)[:, b0:b0 + 4, :].unsqueeze(2), [128, 4, 8, 16]))
        self.tt("dve", v4(TA), pwA(PWR, 0), bq(BBR), ALU.mult)
        self.tt("pool", v4(TBb), pwA(PWI, 0), bq(BBI), ALU.mult)
        self.tt("dve", WTR, TA, TBb, ALU.subtract)
        self.tt("dve", v4(TA), pwA(PWR, 0), bq(BBI), ALU.mult)
        self.tt("pool", v4(TBb), pwA(PWI, 0), bq(BBR), ALU.mult)
        self.tt("dve", WTI, TA, TBb, ALU.add)
        s3 = lambda T: T.v(lambda ap: bc(ap[:, b0:b0 + 4].unsqueeze(2), [128, 4, 128]))
        v3 = lambda T: T.re("p (b x) -> p b x", b=4)
        self.tt("dve", v3(TA), v3(WTR), s3(IA8R), ALU.mult)
        self.tt("pool", v3(TBb), v3(WTI), s3(IA8I), ALU.mult)
        self.tt("dve", PSR, TA, TBb, ALU.subtract)
        self.tt("dve", v3(TA), v3(WTI), s3(IA8R), ALU.mult)
        self.tt("pool", v3(TBb), v3(WTR), s3(IA8I), ALU.mult)
        self.tt("dve", PSI_, TA, TBb, ALU.add)
        self.tt("dve", v4(TA), pwA(PWR, 1), bq(CR), ALU.mult)
        self.tt("pool", v4(TBb), pwA(PWI, 1), bq(CI), ALU.mult)
        self.tt("dve", WOR, TA, TBb, ALU.subtract)
        self.tt("dve", v4(TA), pwA(PWI, 1), bq(CR), ALU.mult)
        self.tt("pool", v4(TBb), pwA(PWR, 1), bq(CI), ALU.mult)
        self.stt(WOI, TA, -1.0, TBb, ALU.mult, ALU.subtract)
        self.ckpt(71)
        for g2 in range(2):
            self.ts("dve", WOBR[g2], WOR, self.HM.sub(g2, g2 + 1), None, ALU.mult)
            self.ts("pool", WOBI[g2], WOI, self.HM.sub(g2, g2 + 1), None, ALU.mult)
        for (src, dstw) in ((WTR, WINR), (WTI, WINI)):
            pbk = self.bank(self.nb("tr", [4, 5]), 512)
            for bb in range(4):
                self.tr(pbk.sub(bb * 128, (bb + 1) * 128), src.sub(bb * 128, (bb + 1) * 128), self.IDENT)
            for g2 in range(2):
                self.memset("pool", dstw[g2], 0.0)
                self.cp("act", dstw[g2].v(lambda ap, g2=g2: ap.rearrange("p (b x) -> p b x", b=4)[:, :, g2 * 64:(g2 + 1) * 64]),
                        pbk.v(lambda ap, g2=g2: ap.rearrange("p (b x) -> p b x", b=4)[:, :, g2 * 64:(g2 + 1) * 64]))
        self.ckpt(72)
        PSRm = [WTR, WTI]
        PSIm = [AR.alloc(512, F32) for _ in range(2)]
        for g2 in range(2):
            self.ts("dve", PSRm[g2], PSR, self.HM.sub(g2, g2 + 1), None, ALU.mult)
            self.ts("pool", PSIm[g2], PSI_, self.HM.sub(g2, g2 + 1), None, ALU.mult)
        for half in range(2):
            pbk = self.bank(self.nb("tr", [4, 5]), 512)
            for q4 in range(4):
                gl = half * 4 + q4
                bb, g2 = gl // 2, gl % 2
                blk = lambda T, bb=bb: T.sub(bb * 128, (bb + 1) * 128)
                self.mm(pbk.sub(q4 * 128, (q4 + 1) * 128), blk(PSRm[g2]), blk(WOR), start=True, stop=False)
                self.mm(pbk.sub(q4 * 128, (q4 + 1) * 128), blk(PSIm[g2]), blk(WOI), start=False, stop=True)
            self.tt("dve", TA.re("p (g x) -> p g x", g=4), pbk.re("p (g x) -> p g x", g=4),
                    MASKT.v(lambda ap: bc(ap.unsqueeze(1), [128, 4, 128])), ALU.mult)
            for q4 in range(4):
                gl = half * 4 + q4
                g = 8 * c + gl
                self.stt(KT.sub(gl * 128, (gl + 1) * 128), IDX, DCOL.sub(g, g + 1), TA.sub(q4 * 128, (q4 + 1) * 128),
                         ALU.mult, ALU.add)
        self.ckpt(7)
        AR.release(mT)
        mL = AR.mark()
        for gl in range(8):
            pbk = self.bank(self.nb("lin", [0, 1, 2, 3]), NJ)
            for r in range(8):
                rhs = UTD.v(lambda ap, r=r: ap.rearrange("p (c r j) -> p c r j", c=KC, r=8)[:, c, r, :])
                self.mm(pbk, Mab(gl, 7 - r), rhs, start=(r == 0), stop=(r == 7))
            self.cp("act", UP.sub(gl * NJ, (gl + 1) * NJ), pbk)
        if self.with_sample:
            for (dstu, rp) in ((UPS0, 7), (UPS7, 0)):
                for half in range(2):
                    pbk = self.bank(self.nb("lin", [0, 1, 2, 3]), 512)
                    for q4 in range(4):
                        gl = half * 4 + q4
                        self.mm(pbk.sub(q4 * 128, (q4 + 1) * 128), Mab(gl, rp), US.sub(c * NS, (c + 1) * NS))
                    self.cp("act", dstu.sub(half * 512, (half + 1) * 512), pbk)
        self.ckpt(8)
        CS = AR.alloc(4 * NJ, F32)
        SN = AR.alloc(4 * NJ, F32)
        c3 = lambda T: T.re("p (b j) -> p b j", b=4)
        self.cp("dve", c3(CS).v(lambda ap: ap[:, :, 0:1]), E1R.v(lambda ap: ap[:, b0:b0 + 4].unsqueeze(2)))
        self.cp("dve", c3(SN).v(lambda ap: ap[:, :, 0:1]), E1I.v(lambda ap: ap[:, b0:b0 + 4].unsqueeze(2)))
        D1 = AR.alloc(2 * NJ, F32)
        D2 = AR.alloc(2 * NJ, F32)
        for k in range(LOGJ):
            n = 1 << k
            sl = lambda T, a, b_: c3(T).v(lambda ap: ap[:, :, a:b_])
            mlt = lambda T: c3(T).v(lambda ap: bc(ap[:, :, n - 1:n], [128, 4, n]))
            t1 = D1.sub(0, 4 * n).re("p (b j) -> p b j", b=4)
            t2 = D2.sub(0, 4 * n).re("p (b j) -> p b j", b=4)
            self.tt("dve", t1, sl(CS, 0, n), mlt(CS), ALU.mult)
            self.tt("pool", t2, sl(SN, 0, n), mlt(SN), ALU.mult)
            self.tt("dve", sl(CS, n, 2 * n), t1, t2, ALU.subtract)
            self.tt("dve", t1, sl(CS, 0, n), mlt(SN), ALU.mult)
            self.tt("pool", t2, sl(SN, 0, n), mlt(CS), ALU.mult)
            self.tt("dve", sl(SN, n, 2 * n), t1, t2, ALU.add)
        SRE = AR.alloc(2 * NJ, F32)
        SIM = AR.alloc(2 * NJ, F32)
        WR = AR.alloc(2 * NJ, F32)
        WI = AR.alloc(2 * NJ, F32)
        ZR, ZI = SRE, SIM
        for pr in range(2):
            for (winT, sdst) in ((WINR, SRE), (WINI, SIM)):
                pbk = self.bank(self.nb("lin", [0, 1, 2, 3]), 2 * NJ)
                for q2 in range(2):
                    bb = pr * 2 + q2
                    for g2 in range(2):
                        gl = 2 * bb + g2
                        self.mm(pbk.sub(q2 * NJ, (q2 + 1) * NJ), winT[g2].sub(bb * 128, (bb + 1) * 128),
                                UP.sub(gl * NJ, (gl + 1) * NJ), start=(g2 == 0), stop=(g2 == 1))
                self.cp("act", sdst, pbk)
            csp = CS.sub(pr * 2 * NJ, (pr + 1) * 2 * NJ)
            snp = SN.sub(pr * 2 * NJ, (pr + 1) * 2 * NJ)
            self.tt("dve", D1, csp, SRE, ALU.mult)
            self.tt("pool", D2, snp, SIM, ALU.mult)
            self.tt("dve", WR, D1, D2, ALU.add)
            self.tt("dve", D1, csp, SIM, ALU.mult)
            self.tt("pool", D2, snp, SRE, ALU.mult)
            self.tt("dve", WI, D1, D2, ALU.subtract)
            for q2 in range(2):
                bg = b0 + pr * 2 + q2
                rho = RHO8.v(lambda ap, bg=bg: bc(ap[:, bg:bg + 1], [128, NJ]))
                for (wsrc, zdst) in ((WR, ZR), (WI, ZI)):
                    ws_, zd_ = wsrc.sub(q2 * NJ, (q2 + 1) * NJ), zdst.sub(q2 * NJ, (q2 + 1) * NJ)
                    P.op("dve", lambda e, rho=rho, ws_=ws_, zd_=zd_: e.tensor_tensor_scan(
                        zd_.ap, rho.ap, ws_.ap, 0.0, ALU.mult, ALU.add), reads=[rho, ws_], writes=[zd_])
            self.tt("dve", D1, csp, ZR, ALU.mult)
            self.tt("pool", D2, snp, ZI, ALU.mult)
            self.tt("dve", WR, D1, D2, ALU.subtract)
            self.tt("dve", D1, csp, ZI, ALU.mult)
            self.tt("pool", D2, snp, ZR, ALU.mult)
            self.tt("dve", WI, D1, D2, ALU.add)
            for (gsrc, fin, hb) in ((WR, FINR, HBR), (WI, FINI, HBI)):
                g3 = gsrc.re("p (b j) -> p b j", b=2)
                self.cp("dve", fin.v(lambda ap: ap[:, b0 + pr * 2:b0 + pr * 2 + 2].unsqueeze(2)),
                        g3.v(lambda ap: ap[:, :, NJ - 1:NJ]))
                h3 = hb.sub(pr * 2 * NJ, (pr + 1) * 2 * NJ).re("p (b j) -> p b j", b=2)
                self.memset("pool", h3.v(lambda ap: ap[:, :, 0:1]), 0.0)
                self.cp("act", h3.v(lambda ap: ap[:, :, 1:NJ]), g3.v(lambda ap: ap[:, :, 0:NJ - 1]))
        self.ckpt(9)
        AR.release(mL)
        mS = AR.mark()
        if self.with_sample:
            SSR = AR.alloc(4 * NS, F32)
            SSI = AR.alloc(4 * NS, F32)
            H0N = AR.alloc(4 * NS, F32)
            E1_ = AR.alloc(4 * NS, F32)
            E2_ = AR.alloc(4 * NS, F32)
            for (st, h0, h0b) in ((self.st_re, H0R, H0BR), (self.st_im, H0I, H0BI)):
                self.dma("sp", H0N, st[j, :, b0 * 128:(b0 + 4) * 128], "h0load")
                pbk = self.bank(self.nb("tr", [4, 5]), 512)
                for bb in range(4):
                    self.tr(pbk.sub(bb * 128, (bb + 1) * 128), H0N.sub(bb * 128, (bb + 1) * 128), self.IDENT)
                self.cp("act", h0, pbk)
                self.cp("dve", h0b, pbk)
            self.ckpt(91)
            for (winT, sdst) in ((WINR, SSR), (WINI, SSI)):
                pbk = self.bank(self.nb("lin", [0, 1, 2, 3]), 512)
                for bb in range(4):
                    for g2 in range(2):
                        gl = 2 * bb + g2
                        self.mm(pbk.sub(bb * NS, (bb + 1) * NS), winT[g2].sub(bb * 128, (bb + 1) * 128),
                                UPS7.sub(gl * NS, (gl + 1) * NS), start=(g2 == 0), stop=(g2 == 1))
                self.cp("act", sdst, pbk)
            self.ckpt(92)
            a3 = lambda T: T.v(lambda ap: bc(ap.rearrange("p (b k) -> p b k", k=9)[:, b0:b0 + 4, 1:2], [128, 4, NS]))
            h3 = lambda T: T.re("p (b s) -> p b s", b=4)
            self.tt("dve", h3(E1_), h3(H0R), a3(PWR), ALU.mult)
            self.tt("pool", h3(E2_), h3(H0I), a3(PWI), ALU.mult)
            self.tt("dve", E1_, E1_, E2_, ALU.subtract)
            self.tt("dve", HNR, E1_, SSR, ALU.add)
            self.tt("dve", h3(E1_), h3(H0I), a3(PWR), ALU.mult)
            self.tt("pool", h3(E2_), h3(H0R), a3(PWI), ALU.mult)
            self.tt("dve", E1_, E1_, E2_, ALU.add)
            self.tt("dve", HNI, E1_, SSI, ALU.add)
            self.ckpt(93)
            for (hn, od) in ((HNR, self.o_re_s), (HNI, self.o_im_s)):
                pbk = self.bank(self.nb("tr", [4, 5]), 512)
                for bb in range(4):
                    self.tr(pbk.sub(bb * 128, (bb + 1) * 128), hn.sub(bb * 128, (bb + 1) * 128), self.IDENT)
                self.cp("act", H0N, pbk)
                self.dma("sp", od[j, :, b0 * 128:(b0 + 4) * 128], H0N, "h0store", is_out=True)
        self.ckpt(94)
        YT = AR.alloc(512, F32)
        jobs = [(UP, HBR, HBI, YH, YL, NJ)]
        if self.with_sample:
            jobs.append((UPS0, H0BR, H0BI, YHS, YLS, NS))
        for (uu, hr, hi, yh, yl, n) in jobs:
            per = 512 // n
            for g0 in range(0, 8, per):
                pbk = self.bank(self.nb("lin", [0, 1, 2, 3]), 512)
                for q in range(per):
                    gl = g0 + q
                    bb, g2 = gl // 2, gl % 2
                    out = pbk.sub(q * n, (q + 1) * n)
                    self.mm(out, KT.sub(gl * 128, (gl + 1) * 128), uu.sub(gl * n, (gl + 1) * n), start=True, stop=False)
                    wr_ = WOBR[g2].sub(bb * 128, (bb + 1) * 128)
                    wi_ = WOBI[g2].sub(bb * 128, (bb + 1) * 128)
                    hr_ = hr.sub(bb * n, (bb + 1) * n)
                    hi_ = hi.sub(bb * n, (bb + 1) * n)
                    self.mm(out, wr_, hr_, start=False, stop=False)
                    self.mm(out, wi_, hi_, start=False, stop=True)
                yhs = yh.sub(g0 * n, g0 * n + 512)
                self.cp("act", yhs, pbk)
                self.tt("dve", YT, pbk, yhs, ALU.subtract)
                self.cp("pool", yl.sub(g0 * n, g0 * n + 512), YT)
        self.ckpt(10)
        for t in range(8):
            pbk = self.bank(self.nb("lin", [0, 1, 2, 3]), NJ)
            i = 0
            for gl in range(8):
                for ysrc in (YH, YL):
                    self.mm(pbk, Mab(t, gl), ysrc.sub(gl * NJ, (gl + 1) * NJ), start=(i == 0), stop=(i == 15))
                    i += 1
            dst = Z.v(lambda ap, t=t: ap.rearrange("p (c j r) -> p c j r", c=KC, r=8)[:, c, :, t])
            self.act(dst, pbk, AF.Gelu_apprx_tanh)
        if self.with_sample:
            pbk = self.bank(self.nb("lin", [0, 1, 2, 3]), NS)
            i = 0
            for gl in range(8):
                for ysrc in (YHS, YLS):
                    self.mm(pbk, Mab(0, gl), ysrc.sub(gl * NS, (gl + 1) * NS), start=(i == 0), stop=(i == 15))
                    i += 1
            self.act(ZS.sub(c * NS, (c + 1) * NS), pbk, AF.Gelu_apprx_tanh)
        AR.release(mS)
    AR.release(mP)
    self.ckpt(11)
    for (fin, od) in ((FINR, self.o_re_p), (FINI, self.o_im_p)):
        pbk = self.bank(6, 128)
        self.tr(pbk.v(lambda ap: ap[0:32, :]), fin, self.IDENT)
        st_ = AR.alloc(128, F32)
        self.cp("act", st_.v(lambda ap: ap[0:32, :]), pbk.v(lambda ap: ap[0:32, :]))
        self.dma("sp", od[j], st_.v(lambda ap: ap[0:32, :]), "finstore", is_out=True)
    self.ckpt(12)
    mG = AR.mark()
    SL = [AR.alloc(KC * 256, BF16) for _ in range(2)]
    O = AR.alloc(KC * 512, F32)
    VAL = AR.alloc(KC * 512, F32)
    SGt = [AR.alloc(512, F32) for _ in range(2)]
    sc = self.alloc_norm_scratch()
    for (xfn, n, tag) in self.tiles():
        if tag[0] == "p":
            t0 = tag[1]
            rhs = lambda k, t0=t0: Z.sub(k * NT + t0, k * NT + t0 + 512)
        else:
            rhs = lambda k: ZS.sub(k * NS, (k + 1) * NS)

        def consume(oc, pb, n=n):
            if oc < KC:
                self.cp("act", VAL.sub(oc * 512, oc * 512 + n), pb)
            else:
                o2 = oc - KC
                sg = SGt[o2 % 2].sub(0, n)
                self.act(sg, pb, AF.Sigmoid)
                self.tt("dve", O.sub(o2 * 512, o2 * 512 + n), VAL.sub(o2 * 512, o2 * 512 + n), sg, ALU.mult)
        self.linear_fm(self.w_glu[j], 0, 2 * KC, rhs, n, consume, SL, "gluw")
        self.norm_add(lambda c, n=n: O.sub(c * 512, c * 512 + n), n, 1, l, xfn, sc)
    AR.release(mG)
    AR.release(m0)


K.s5_layer = s5_layer


def build_program(NT=2048, layers=(0, 1, 2, 3), with_sample=True, do_ffn=True, n_cores=8, pool_pages=2560, stop=0):
    k = K(NT, layers, n_cores=n_cores, with_sample=with_sample, pool_pages=pool_pages)
    k.stop = stop
    k._declare_io()
    k._setup_consts()
    k._load_x()
    try:
        k._layers(layers, do_ffn)
    except StopBuild:
        k.AR.top = 0
    k._store_y()
    k.P.finish()
    return k


def _layers(k, layers, do_ffn):
    for l in layers:
        kind, j = l % 3, l // 3
        if kind == 0:
            k.s5_layer(l, j)
        elif kind == 1:
            k.sb_layer(l, j)
        else:
            k.cm_layer(l, j)
        if do_ffn:
            k.ffn(l)


K._layers = _layers


_POOL_CACHE = {}


def make_in_map(inp, core, xp_override=None, small=False, pool_pages=2560):
    f32 = np.float32
    c2 = slice(2 * core, 2 * core + 2)
    m = {}
    m["xp"] = np.ascontiguousarray(xp_override if xp_override is not None else inp["x_prompt"][core], f32)
    m["xs"] = np.ascontiguousarray(inp["x_sample"][:, 0, :], f32)
    m["st_re"] = np.ascontiguousarray(inp["state_ssm_re"], f32).reshape(2, NS, 4096)
    m["st_im"] = np.ascontiguousarray(inp["state_ssm_im"], f32).reshape(2, NS, 4096)
    if pool_pages == 2560:
        if "pools" not in _POOL_CACHE:
            ck = inp["cache_k"][0]
            cv = inp["cache_v"][0]
            kq = np.ascontiguousarray(np.transpose(ck.reshape(2560, 128, 8, 128), (0, 2, 3, 1)), f32)
            vq = np.ascontiguousarray(np.transpose(cv.reshape(2560, 128, 8, 128), (0, 2, 1, 3)), f32)
            _POOL_CACHE["pools"] = (kq.reshape(-1, 128), vq.reshape(-1, 128))
        m["kpool"], m["vpool"] = _POOL_CACHE["pools"]
    else:
        m["kpool"] = np.zeros((pool_pages * 1024, 128), f32)
        m["vpool"] = np.zeros((pool_pages * 1024, 128), f32)
    m["ptab"] = np.ascontiguousarray(inp["page_table"], np.int32)
    for a, b in (("n_mix_pre", "norm_mix_pre"), ("n_mix_post", "norm_mix_post"), ("n_ffn_pre", "norm_ffn_pre"),
                 ("n_ffn_post", "norm_ffn_post"), ("w_ffn_in", "w_ffn_in"), ("w_ffn_out", "w_ffn_out"),
                 ("w_ssm_in", "w_ssm_in"), ("lam_re", "ssm_lambda_re"), ("lam_im", "ssm_lambda_im"),
                 ("log_dt", "ssm_log_dt"), ("b_re", "ssm_b_re"), ("b_im", "ssm_b_im"), ("c_re", "ssm_c_re"),
                 ("c_im", "ssm_c_im"), ("ssm_d", "ssm_d"), ("w_glu", "w_ssm_glu"), ("w_qkv", "w_sb_qkv"),
                 ("w_sbo", "w_sb_out"), ("sb_bias", "sb_logit_bias"), ("w_cm_in", "w_cm_in"), ("cm_vn", "cm_v_norm"),
                 ("cm_ws", "cm_w_s"), ("cm_bs", "cm_b_s"), ("w_cm_out", "w_cm_out")):
        m[a] = np.ascontiguousarray(inp[b], f32)
    return m


def cm_layer(self, l, j):
    AR, P = self.AR, self.P
    NT = self.NT
    m0 = AR.mark()
    WU = AR.alloc(KC * D, BF16)
    WV = AR.alloc(KC * D, BF16)
    WO = AR.alloc(KC * D, BF16)
    wv = self.w_cm_in[j].rearrange("(k p) n -> p k n", p=128)
    for k in range(KC):
        self.wslab(WU.sub(k * D, (k + 1) * D), self.w_cm_in[j][k * 128:(k + 1) * 128, 0:D], "cmw")
        self.wslab(WV.sub(k * D, (k + 1) * D), self.w_cm_in[j][k * 128:(k + 1) * 128, D:2 * D], "cmw")
        self.wslab(WO.sub(k * D, (k + 1) * D), self.w_cm_out[j][k * 128:(k + 1) * 128, :], "cmw")
    VG = AR.alloc(D, F32)
    self.dma("sp", VG, self.cm_vn[j].rearrange("(o n) -> o n", o=1).broadcast_to([128, D]), "cmc")
    BS = AR.alloc(D, F32)
    self.dma("sp", BS.v(lambda ap: ap[0:1, :]), self.cm_bs[j].rearrange("(o g) t -> o (g t)", o=1), "cmc")
    ONE1 = AR.alloc(128, F32)
    self.memset("dve", ONE1, 1.0)
    AB = AR.alloc(16, F32)
    self.dma("sp", AB.sub(0, 8), self.cm_ws[j][:, 0, 0:1].rearrange("g o -> o g").broadcast_to([128, 8]), "cmc", slow=True)
    self.dma("sp", AB.sub(8, 16), self.cm_bs[j][:, 0:1].rearrange("g o -> o g").broadcast_to([128, 8]), "cmc", slow=True)
    WST = AR.alloc(8 * 128, BF16)
    mW = AR.mark()
    WSN = AR.alloc(8 * 128, F32)
    WSM = AR.alloc(8 * 128, F32)
    self.dma("sp", WSN.re("p (g s) -> p g s", g=8), self.cm_ws[j].rearrange("g t s -> t g s"), "cmc")
    P.op("pool", lambda e: e.affine_select(WSM.ap, WSN.ap, [[0, 8], [-1, 128]], ALU.is_ge, 0.0, base=0,
                                           channel_multiplier=1), reads=[WSN], writes=[WSM])
    for half in range(2):
        pbk = self.bank(self.nb("tr", [4, 5]), 512)
        for q in range(4):
            g = half * 4 + q
            self.tr(pbk.sub(q * 128, (q + 1) * 128), WSM.sub(g * 128, (g + 1) * 128), self.IDENT)
        self.cp("act", WST.sub(half * 512, (half + 1) * 512), pbk)
    AR.release(mW)
    XN = AR.alloc(KC * 512, BF16)
    U = AR.alloc(KC * 512, F32)
    O = U
    UM = AR.alloc(KC * 512, BF16)
    VT = AR.alloc(D, F32)
    VSQ = AR.alloc(D, F32)
    VLN = AR.alloc(D, F32)
    VB = AR.alloc(D, BF16)
    ST = AR.alloc(16, F32)
    sc = self.alloc_norm_scratch()
    for (xfn, n, tag) in self.tiles():
        self.norm_to(xfn, n, 0, l, lambda c, n=n: XN.sub(c * 512, c * 512 + n), sc)
        for oc in range(KC):
            pbk = self.bank(self.nb("lin", [0, 1, 2, 3]), n)
            for k in range(KC):
                self.mm(pbk, WU.sub(k * D + oc * 128, k * D + (oc + 1) * 128), XN.sub(k * 512, k * 512 + n),
                        start=(k == 0), stop=(k == KC - 1))
            self.act(U.sub(oc * 512, oc * 512 + n), pbk, AF.Gelu_apprx_tanh)
        for ch in range(n // 128):
            for hf in range(2):
                pbk = self.bank(self.nb("lin", [0, 1, 2, 3]), 512)
                for k in range(KC):
                    self.mm(pbk, XN.sub(k * 512 + ch * 128, k * 512 + (ch + 1) * 128),
                            WV.sub(k * D + hf * 512, k * D + (hf + 1) * 512), start=(k == 0), stop=(k == KC - 1))
                self.act(VT.sub(hf * 512, (hf + 1) * 512), pbk, AF.Gelu_apprx_tanh, accum=ST.sub(hf, hf + 1))
                self.act(VSQ.sub(hf * 512, (hf + 1) * 512), VT.sub(hf * 512, (hf + 1) * 512), AF.Square,
                         accum=ST.sub(2 + hf, 3 + hf))
            self.tt("dve", ST.sub(4, 5), ST.sub(0, 1), ST.sub(1, 2), ALU.add)
            self.tt("dve", ST.sub(5, 6), ST.sub(2, 3), ST.sub(3, 4), ALU.add)
            self.ts("dve", ST.sub(6, 7), ST.sub(4, 5), 1.0 / D, None, ALU.mult)
            self.tt("dve", ST.sub(7, 8), ST.sub(6, 7), ST.sub(6, 7), ALU.mult)
            self.stt(ST.sub(8, 9), ST.sub(5, 6), 1.0 / D, ST.sub(7, 8), ALU.mult, ALU.subtract)
            self.act(ST.sub(9, 10), ST.sub(8, 9), AF.Ln, bias=self.EPSB.sub(0, 1))
            self.act(ST.sub(10, 11), ST.sub(9, 10), AF.Exp, scale=-0.5)
            self.ts("dve", VSQ, VT, ST.sub(6, 7), ST.sub(10, 11), ALU.subtract, ALU.mult)
            self.tt("pool", VLN, VSQ, VG, ALU.mult)
            if tag[0] == "p":
                self.cp("act", VB, VLN)
                for half in range(2):
                    pbk = self.bank(self.nb("tr", [4, 5]), 512)
                    for q in range(4):
                        g = half * 4 + q
                        o_ = pbk.sub(q * 128, (q + 1) * 128)
                        self.mm(o_, VB.sub(g * 128, (g + 1) * 128), WST.sub(g * 128, (g + 1) * 128), start=True, stop=False)
                        self.mm(o_, ONE1.v(lambda ap: ap[0:1, :]), BS.v(lambda ap, g=g: ap[0:1, g * 128:(g + 1) * 128]),
                                start=False, stop=True)
                    uv = U.v(lambda ap, half=half, ch=ch: ap.rearrange("p (g t) -> p g t", g=KC)[:, half * 4:half * 4 + 4, ch * 128:(ch + 1) * 128])
                    umv = UM.v(lambda ap, half=half, ch=ch: ap.rearrange("p (g t) -> p g t", g=KC)[:, half * 4:half * 4 + 4, ch * 128:(ch + 1) * 128])
                    self.tt("dve", umv, uv, pbk.re("p (g t) -> p g t", g=4), ALU.mult)
            else:
                self.dma("sp", self.o_cmv, VLN, "cmvout", is_out=True)
                for half in range(2):
                    pbk = self.bank(self.nb("tr", [4, 5]), 512)
                    for q in range(4):
                        g = half * 4 + q
                        self.tr(pbk.sub(q * 128, (q + 1) * 128), VLN.sub(g * 128, (g + 1) * 128), self.IDENT)
                    for q in range(4):
                        g = half * 4 + q
                        mx = VSQ.sub(q * 128, (q + 1) * 128)
                        self.ts("dve", mx, pbk.sub(q * 128, (q + 1) * 128), AB.sub(g, g + 1), AB.sub(8 + g, 9 + g), ALU.mult, ALU.add)
                        self.tt("dve", UM.sub(g * 512, g * 512 + NS), U.sub(g * 512, g * 512 + NS), mx, ALU.mult)
        for oc in range(KC):
            pbk = self.bank(self.nb("lin", [0, 1, 2, 3]), n)
            for g in range(KC):
                self.mm(pbk, WO.sub(g * D + oc * 128, g * D + (oc + 1) * 128), UM.sub(g * 512, g * 512 + n),
                        start=(g == 0), stop=(g == KC - 1))
            self.cp("act", O.sub(oc * 512, oc * 512 + n), pbk)
        self.norm_add(lambda c, n=n: O.sub(c * 512, c * 512 + n), n, 1, l, xfn, sc)
    AR.release(m0)


K.cm_layer = cm_layer


def sb_layer(self, l, j):
    AR, P = self.AR, self.P
    NT = self.NT
    NTI = NT // 128
    m0 = AR.mark()
    XN = AR.alloc(KC * NT, BF16)
    XNS = AR.alloc(KC * NS, BF16)
    OT = AR.alloc(KC * NT, BF16)
    OTS = AR.alloc(KC * NS, BF16)
    TRI = AR.alloc(128, BF16)
    BIASC = AR.alloc(16, F32)
    ONEC = AR.alloc(8, F32)
    self.ONEC = ONEC
    self.memset("dve", ONEC, 1.0)
    self.dma("sp", BIASC, self.sb_bias[j].rearrange("(o h) -> o h", o=1).broadcast_to([128, 16]), "sbc", slow=True)
    mt = AR.mark()
    PMC = AR.alloc(128, F32)
    P.op("pool", lambda e: e.iota(PMC.ap, [[-1, 128]], base=0, channel_multiplier=1,
                                  allow_small_or_imprecise_dtypes=True), writes=[PMC])
    self.ts("dve", TRI, PMC, 0.0, None, ALU.is_ge)
    sc = self.alloc_norm_scratch()
    for (xfn, n, tag) in self.tiles():
        if tag[0] == "p":
            t0 = tag[1]
            self.norm_to(xfn, n, 0, l, lambda c, t0=t0: XN.sub(c * NT + t0, c * NT + t0 + 512), sc)
        else:
            self.norm_to(xfn, n, 0, l, lambda c: XNS.sub(c * NS, (c + 1) * NS), sc)
    AR.release(mt)
    mB = AR.mark()
    SLB = [AR.alloc(KC * 512, BF16) for _ in range(2)]
    STG = [AR.alloc(512, F32) for _ in range(2)]
    wv3 = self.w_qkv[j].rearrange("(k p) n -> p k n", p=128)
    for cs in range(4):
        si = self.nb("sbw", [0, 1])
        slab = SLB[si]
        self.wslab(slab.re("p (k n) -> p k n", k=KC), wv3[:, :, D + cs * 512: D + (cs + 1) * 512], "sbw%d" % si)
        jobs = [(XN, NT, tt, (self.o_kp if cs < 2 else self.o_vp)) for tt in range(NTI)]
        if self.with_sample:
            jobs.append((XNS, NS, 0, (self.o_ks if cs < 2 else self.o_vs)))
        for (xn, nn, tt, od) in jobs:
            pbk = self.bank(self.nb("lin", [0, 1, 2, 3]), 512)
            for k in range(KC):
                self.mm(pbk, xn.sub(k * nn + tt * 128, k * nn + (tt + 1) * 128), slab.sub(k * 512, (k + 1) * 512),
                        start=(k == 0), stop=(k == KC - 1))
            sg = self.nb("sbstg", [0, 1])
            self.cp("act" if sg == 0 else "dve", STG[sg], pbk)
            self.dma("sp", od[tt * 128:(tt + 1) * 128, (cs % 2) * 512:(cs % 2 + 1) * 512], STG[sg], "sbkv%d" % sg, is_out=True)
    AR.release(mB)
    mC = AR.mark()
    WQ = [AR.alloc(KC * 128, BF16) for _ in range(2)]
    WK = [AR.alloc(KC * 128, BF16) for _ in range(2)]
    WV = [AR.alloc(KC * 128, BF16) for _ in range(2)]
    QT = AR.alloc(NT, BF16)
    KTm = [AR.alloc(NT, BF16) for _ in range(2)]
    Vm = [AR.alloc(NT, BF16) for _ in range(2)]
    E = [AR.alloc(512, F32) for _ in range(2)]
    LB = [AR.alloc(512, BF16) for _ in range(2)]
    EX = [AR.alloc(512, F32) for _ in range(2)]
    WB = [AR.alloc(512, BF16) for _ in range(2)]
    RB = AR.alloc(512, BF16)
    for c in range(KC):
        si = c % 2
        for (wt, col0) in ((WQ[si], 0), (WK[si], D), (WV[si], 2 * D)):
            self.wslab(wt.re("p (k n) -> p k n", k=KC), wv3[:, :, col0 + c * 128: col0 + (c + 1) * 128], "sbq%d" % si)
        for t0 in range(0, NT, 512):
            pq = self.bank(self.nb("lin", [0, 1, 2, 3]), 512)
            for k in range(KC):
                self.mm(pq, WQ[si].sub(k * 128, (k + 1) * 128), XN.sub(k * NT + t0, k * NT + t0 + 512), start=(k == 0), stop=(k == KC - 1))
            self.act(QT.sub(t0, t0 + 512), pq, AF.Copy, scale=0.125)
            pk = self.bank(self.nb("lin", [0, 1, 2, 3]), 512)
            for k in range(KC):
                self.mm(pk, WK[si].sub(k * 128, (k + 1) * 128), XN.sub(k * NT + t0, k * NT + t0 + 512), start=(k == 0), stop=(k == KC - 1))
            for h in range(2):
                self.ts("dve", KTm[h].sub(t0, t0 + 512), pk, self.HM.sub(h, h + 1), None, ALU.mult)
        for h in range(2):
            self.memset("pool", Vm[h], 0.0)
        for t4 in range(0, NTI, 4):
            pv = self.bank(self.nb("lin", [0, 1, 2, 3]), 512)
            for q in range(4):
                tt = t4 + q
                for k in range(KC):
                    self.mm(pv.sub(q * 128, (q + 1) * 128), XN.sub(k * NT + tt * 128, k * NT + (tt + 1) * 128),
                            WV[si].sub(k * 128, (k + 1) * 128), start=(k == 0), stop=(k == KC - 1))
            for h in range(2):
                dst = Vm[h].v(lambda ap, h=h, t4=t4: ap.rearrange("p (t x) -> p t x", x=128)[:, t4:t4 + 4, h * 64:(h + 1) * 64])
                src = pv.v(lambda ap, h=h: ap.rearrange("p (t x) -> p t x", x=128)[:, :, h * 64:(h + 1) * 64])
                self.cp("act", dst, src)
        for Qg in range(NT // 512):
            po = self.bank(self.nb("sbo", [6, 5]), 512)
            first_o = True
            for h in range(2):
                bcol = BIASC.sub(2 * c + h, 2 * c + h + 1)
                nkb = 4 * Qg + 4
                for ki, kb in enumerate(range(nkb - 1, -1, -1)):
                    s2 = self.nb("sbrot", [0, 1])
                    pz = self.bank(self.nb("sbz", [0, 1]), 512)
                    self.mm(pz, KTm[h].sub(kb * 128, (kb + 1) * 128), QT.sub(Qg * 512, (Qg + 1) * 512))
                    e_ = E[s2]
                    self.act(e_, pz, AF.Exp, bias=bcol)
                    if kb >= 4 * Qg:
                        base = Qg * 512 - kb * 128
                        P.op("pool", lambda e, e_=e_, base=base: e.affine_select(
                            e_.ap, e_.ap, [[1, 512]], ALU.is_gt, 0.0, base=base, channel_multiplier=-1),
                            reads=[e_], writes=[e_])
                    lb = LB[s2]
                    self.act(lb, e_, AF.Ln, bias=ONEC.sub(0, 1))
                    ps_ = self.bank(self.nb("sbs", [2, 3]), 512)
                    self.mm(ps_, TRI, lb, start=True, stop=(ki == 0))
                    if ki > 0:
                        self.mm(ps_, self.ONESB, RB, start=False, stop=True)
                    ex = EX[s2]
                    self.act(ex, ps_, AF.Exp, scale=-1.0)
                    wb = WB[s2]
                    self.tt("dve", wb, e_, ex, ALU.mult)
                    if ki == 0:
                        self.cp("pool", RB, lb)
                    else:
                        self.tt("pool", RB, RB, lb, ALU.add)
                    last_o = (h == 1 and kb == 0)
                    self.mm(po, Vm[h].sub(kb * 128, (kb + 1) * 128), wb, start=first_o, stop=last_o)
                    first_o = False
            self.cp("act", OT.sub(c * NT + Qg * 512, c * NT + (Qg + 1) * 512), po)
    AR.release(mC)
    if self.with_sample:
        self.sb_sample_attn(l, j, XNS, OTS, TRI, BIASC)
    mD = AR.mark()
    SL = [AR.alloc(KC * 256, BF16) for _ in range(2)]
    O = AR.alloc(KC * 512, F32)
    sc = self.alloc_norm_scratch()
    for (xfn, n, tag) in self.tiles():
        if tag[0] == "p":
            t0 = tag[1]
            rhs = lambda k, t0=t0: OT.sub(k * NT + t0, k * NT + t0 + 512)
        else:
            rhs = lambda k: OTS.sub(k * NS, (k + 1) * NS)
        self.linear_fm(self.w_sbo[j], 0, KC, rhs, n,
                       lambda oc, pb, n=n: self.cp("act", O.sub(oc * 512, oc * 512 + n), pb), SL, "sbow")
        self.norm_add(lambda c, n=n: O.sub(c * 512, c * 512 + n), n, 1, l, xfn, sc)
    AR.release(mD)
    AR.release(m0)


def sb_sample_attn(self, l, j, XNS, OTS, TRI, BIASC):
    AR, P, nc = self.AR, self.P, self.nc
    m = AR.mark()
    wv3 = self.w_qkv[j].rearrange("(k p) n -> p k n", p=128)
    WQM = [AR.alloc(KC * 128, BF16) for _ in range(2)]
    QM = AR.alloc(NS, F32)
    QBD = AR.alloc(2 * NS, F32)
    BROW = AR.alloc(256, F32)
    OTM = AR.alloc(NS, F32)
    NB = 16
    KP = [AR.alloc(128, F32) for _ in range(NB)]
    VP = [AR.alloc(128, F32) for _ in range(NB)]
    ZB = AR.alloc(256, F32)
    EE = AR.alloc(256, F32)
    LBs = AR.alloc(256, BF16)
    RT = AR.alloc(256, F32)
    SFX = AR.alloc(256, F32)
    W32 = AR.alloc(256, F32)
    PTB = AR.alloc(NS * 16, I32)
    IDXS = [AR.alloc(NS * 16, I32) for _ in range(2)]
    PIXC = AR.alloc(8, F32)
    self.dma("sp", PTB, self.ptab.rearrange("(o s) g -> o (s g)", o=1).broadcast_to([128, NS * 16]), "sbsc")
    P.op("pool", lambda e: e.iota(PIXC.ap, [[128, 8]], base=0, channel_multiplier=1,
                                  allow_small_or_imprecise_dtypes=True), writes=[PIXC])
    cnt = [0, 0]

    def page_dma(pool2, buf, IDX, idx, key):
        def fn(e, pool2=pool2, buf=buf, idx=idx, IDX=IDX):
            return e.indirect_dma_start(out=buf.ap, out_offset=None, in_=pool2[:, :],
                                        in_offset=bass.IndirectOffsetOnAxis(ap=IDX.ap[:, idx:idx + 1], axis=0))
        P.op("pool", fn, reads=[IDX], writes=[buf], dma_key=key)

    for c in range(KC):
        wq = WQM[c % 2]
        self.wslab(wq.re("p (k n) -> p k n", k=KC), wv3[:, :, c * 128:(c + 1) * 128], "sbqm%d" % (c % 2))
        pq = self.bank(self.nb("lin", [0, 1, 2, 3]), NS)
        for k in range(KC):
            self.mm(pq, wq.sub(k * 128, (k + 1) * 128), XNS.sub(k * NS, (k + 1) * NS), start=(k == 0), stop=(k == KC - 1))
        self.act(QM, pq, AF.Copy, scale=0.125)
        for h in range(2):
            self.ts("dve", QBD.v(lambda ap, h=h: ap.rearrange("p (s h) -> p s h", h=2)[:, :, h]), QM,
                    self.HM.sub(h, h + 1), None, ALU.mult)
        self.cp("dve", BROW.re("p (x h) -> p x h", h=2),
                BIASC.v(lambda ap, c=c: bc(ap[:, 2 * c:2 * c + 2].unsqueeze(1), [128, 128, 2])))
        IDX = IDXS[c % 2]
        self.ts("dve", IDX, PTB, 1024.0, PIXC.sub(c, c + 1), ALU.mult, ALU.add)
        for sg in range(NS // 8):
            pz = self.bank(self.nb("sbz", [0, 1]), 256)
            for s_l in range(8):
                s_ = sg * 8 + s_l
                for pg in range(16):
                    bi = cnt[0] % NB
                    cnt[0] += 1
                    page_dma(self.kpool, KP[bi], IDX, s_ * 16 + pg, "kpg%d" % bi)
                    col = pg * 16 + s_l * 2
                    self.mm(pz.sub(col, col + 2), KP[bi], QBD.sub(s_ * 2, s_ * 2 + 2))
            self.tt("dve", ZB, pz, BROW, ALU.add)
            self.act(EE, ZB, AF.Exp)
            self.act(LBs, EE, AF.Ln, bias=self.ONEC.sub(0, 1))
            ps_ = self.bank(self.nb("sbs", [2, 3]), 512)
            self.mm(ps_.sub(0, 256), TRI, LBs)
            self.mm(ps_.sub(256, 512), self.ONESB, LBs)
            self.memset("dve", RT.sub(15 * 16, 16 * 16), 0.0)
            for pg in range(14, -1, -1):
                self.tt("dve", RT.sub(pg * 16, (pg + 1) * 16), RT.sub((pg + 1) * 16, (pg + 2) * 16),
                        ps_.sub(256 + (pg + 1) * 16, 256 + (pg + 2) * 16), ALU.add)
            self.tt("dve", SFX, ps_.sub(0, 256), RT, ALU.add)
            self.act(SFX, SFX, AF.Exp, scale=-1.0)
            self.tt("dve", W32, EE, SFX, ALU.mult)
            po = self.bank(self.nb("sbo", [6, 5]), 16)
            for s_l in range(8):
                s_ = sg * 8 + s_l
                for pg in range(16):
                    bi = cnt[1] % NB
                    cnt[1] += 1
                    page_dma(self.vpool, VP[bi], IDX, s_ * 16 + pg, "vpg%d" % bi)
                    col = pg * 16 + s_l * 2
                    self.mm(po.sub(s_l * 2, s_l * 2 + 2), VP[bi], W32.sub(col, col + 2), start=(pg == 0), stop=(pg == 15))
            for h in range(2):
                dst = OTM.v(lambda ap, h=h: ap[h * 64:(h + 1) * 64, sg * 8:(sg + 1) * 8])
                src = po.v(lambda ap, h=h: ap[h * 64:(h + 1) * 64, :].rearrange("p (s h) -> p s h", h=2)[:, :, h])
                self.cp("act", dst, src)
        if self.n_cores == 1 and c == 0:
            self.dma("sp", self.o_dbg, OTM, "dbgout", is_out=True)
        self.cp("dve", OTS.sub(c * NS, (c + 1) * NS), OTM)
    AR.release(m)


K.sb_layer = sb_layer
K.sb_sample_attn = sb_sample_attn


_PROG_CACHE = {}


def kernel(**inputs):
    n = 8
    inp = {k: np.asarray(v) for k, v in inputs.items()}
    if "prog" not in _PROG_CACHE:
        _PROG_CACHE["prog"] = build_program(NT=2048, layers=(0, 1, 2, 3), with_sample=True, do_ffn=True, n_cores=n)
    k = _PROG_CACHE["prog"]
    in_maps = []
    for c in range(n):
        m = make_in_map(inp, c)
        in_maps.append({kk: m[kk] for kk in k.inputs})
    _POOL_CACHE.clear()
    res = run_bass_kernel_spmd(k.nc, in_maps, core_ids=list(range(n)))
    del in_maps
    r = res.results
    f32 = np.float32
    y_prompt = np.stack([r[c]["yp"] for c in range(n)], 0).astype(f32)
    y_sample = r[0]["ys"].reshape(NS, 1, D).astype(f32)
    re_p = np.stack([r[c]["o_re_p"].reshape(2, 64, 64) for c in range(n)], 1).astype(f32)
    im_p = np.stack([r[c]["o_im_p"].reshape(2, 64, 64) for c in range(n)], 1).astype(f32)
    re_s = r[0]["o_re_s"].reshape(2, NS, 64, 64).astype(f32)
    im_s = r[0]["o_im_s"].reshape(2, NS, 64, 64).astype(f32)
    k_p = np.stack([r[c]["o_kp"].reshape(2048, 16, 64) for c in range(n)], 0)[None].astype(f32)
    v_p = np.stack([r[c]["o_vp"].reshape(2048, 16, 64) for c in range(n)], 0)[None].astype(f32)
    k_s = r[0]["o_ks"].reshape(1, NS, 1, 16, 64).astype(f32)
    v_s = r[0]["o_vs"].reshape(1, NS, 1, 16, 64).astype(f32)
    cmv = r[0]["o_cmv"].reshape(1, NS, 1, D).astype(f32)
    return (y_prompt, y_sample, re_p, im_p, re_s, im_s, k_p, v_p, k_s, v_s, cmv)
```
